# Optimizing a Trainium2 kernel written in Bass

```python
import math
import jax, jax.numpy as jnp
from jax import lax
import numpy as np

D_MODEL = 2048
BATCH = 4
SEQ = 2048
DEPTH = 1

N_HEADS_DIFF = 8
DIFF_HEAD_DIM = 64
DIFF_WIDTH = N_HEADS_DIFF * 2 * DIFF_HEAD_DIM
N_HEADS_MOBA = 8
MOBA_HEAD_DIM = 128
MOBA_WIDTH = N_HEADS_MOBA * MOBA_HEAD_DIM
MOBA_BLOCK = 256
MOBA_TOPK = 3
MOBA_Q_CHUNK = 32
ATTN_Q_BLOCK = 128
D_FF = 5632
RMS_EPS = 1e-6
IN_COLS = 3 * DIFF_WIDTH + 3 * MOBA_WIDTH + 2 * D_MODEL

kernel_name = "hybrid_diffattn_moba_gated_macaron"


def rmsnorm(x, g):
    xf = x.astype(jnp.float32)
    y = xf * lax.rsqrt(jnp.mean(xf * xf, axis=-1, keepdims=True) + RMS_EPS)
    return (y * g.astype(jnp.float32)).astype(x.dtype)


def swiglu(h, w_gu, w_down):
    gate, up = jnp.split(h @ w_gu, 2, axis=-1)
    return (jax.nn.silu(gate) * up) @ w_down


def alibi_slopes(n):
    return jnp.asarray(2.0 ** (-8.0 * np.arange(1, n + 1) / n), dtype=jnp.float32)


def diff_attention(q, k, v, lam, subln_g, lam_init):
    B, S, H, _, dh = q.shape
    q = q.transpose(0, 2, 3, 1, 4)
    k = k.transpose(0, 2, 3, 1, 4)
    v = v.transpose(0, 2, 1, 3)
    nq = S // ATTN_Q_BLOCK
    qb = q.reshape(B, H, 2, nq, ATTN_Q_BLOCK, dh).transpose(3, 0, 1, 2, 4, 5)
    slopes = alibi_slopes(H)
    kpos = jnp.arange(S)
    scale = dh ** -0.5

    def block(args):
        qblk, start = args
        s = jnp.einsum('bhmqd,bhmkd->bhmqk', qblk, k).astype(jnp.float32) * scale
        qpos = start + jnp.arange(ATTN_Q_BLOCK)
        dist = (qpos[:, None] - kpos[None, :]).astype(jnp.float32)
        s = s - (slopes[:, None, None] * dist)[None, :, None]
        s = jnp.where(dist >= 0, s, -jnp.inf)
        p = jax.nn.softmax(s, axis=-1)
        a = p[:, :, 0] - lam * p[:, :, 1]
        return jnp.einsum('bhqk,bhkd->bhqd', a.astype(v.dtype), v)

    starts = jnp.arange(nq, dtype=jnp.int32) * ATTN_Q_BLOCK
    o = lax.map(block, (qb, starts))
    o = o.transpose(1, 0, 3, 2, 4).reshape(B, S, H, 2 * dh)
    o = rmsnorm(o, subln_g) * (1.0 - lam_init)
    return o.reshape(B, S, H * 2 * dh)


def moba_attention(q, k, v):
    B, S, H, dh = q.shape
    nb = -(-S // MOBA_BLOCK)
    pad = nb * MOBA_BLOCK - S
    n_sel = min(MOBA_TOPK, nb - 1)
    scale = dh ** -0.5
    slopes = alibi_slopes(H)
    q = q.transpose(0, 2, 1, 3)
    k = jnp.pad(k.transpose(0, 2, 1, 3), ((0, 0), (0, 0), (0, pad), (0, 0)))
    v = jnp.pad(v.transpose(0, 2, 1, 3), ((0, 0), (0, 0), (0, pad), (0, 0)))
    kb = k.reshape(B, H, nb, MOBA_BLOCK, dh)
    vb = v.reshape(B, H, nb, MOBA_BLOCK, dh)
    kmean = jnp.mean(kb.astype(jnp.float32), axis=3)
    qblk = jnp.arange(S) // MOBA_BLOCK
    nc = S // MOBA_Q_CHUNK
    qc_all = q.reshape(B, H, nc, MOBA_Q_CHUNK, dh).transpose(2, 0, 1, 3, 4)
    starts = jnp.arange(nc, dtype=jnp.int32) * MOBA_Q_CHUNK

    if n_sel > 0:
        gscore = jnp.einsum('bhsd,bhnd->bhsn', q.astype(jnp.float32), kmean)
        past = jnp.arange(nb)[None, :] < qblk[:, None]
        gscore = jnp.where(past, gscore, -jnp.inf)
        _, idx = lax.top_k(gscore, n_sel)
        valid = idx < qblk[:, None]
        idx_all = idx.reshape(B, H, nc, MOBA_Q_CHUNK, n_sel).transpose(2, 0, 1, 3, 4)
        val_all = valid.reshape(B, H, nc, MOBA_Q_CHUNK, n_sel).transpose(2, 0, 1, 3, 4)
    else:
        idx_all = jnp.zeros((nc, B, H, MOBA_Q_CHUNK, 0), jnp.int32)
        val_all = jnp.zeros((nc, B, H, MOBA_Q_CHUNK, 0), bool)

    gather_blocks = jax.vmap(jax.vmap(lambda tbl, ix: tbl[ix]))

    def chunk(args):
        qc, idc, vdc, start = args
        qpos = start + jnp.arange(MOBA_Q_CHUNK)
        own = start // MOBA_BLOCK
        k_own = lax.dynamic_index_in_dim(kb, own, axis=2, keepdims=False)
        v_own = lax.dynamic_index_in_dim(vb, own, axis=2, keepdims=False)
        own_pos = own * MOBA_BLOCK + jnp.arange(MOBA_BLOCK)
        d_own = (qpos[:, None] - own_pos[None, :]).astype(jnp.float32)
        s_own = jnp.einsum('bhqd,bhkd->bhqk', qc, k_own).astype(jnp.float32) * scale
        s_own = jnp.where(d_own >= 0, s_own - slopes[None, :, None, None] * d_own, -jnp.inf)
        if n_sel == 0:
            p = jax.nn.softmax(s_own, axis=-1)
            return jnp.einsum('bhqk,bhkd->bhqd', p.astype(v_own.dtype), v_own)
        k_sel = gather_blocks(kb, idc)
        v_sel = gather_blocks(vb, idc)
        sel_pos = idc[..., None] * MOBA_BLOCK + jnp.arange(MOBA_BLOCK)
        d_sel = (qpos[None, None, :, None, None] - sel_pos).astype(jnp.float32)
        s_sel = jnp.einsum('bhqd,bhqjkd->bhqjk', qc, k_sel).astype(jnp.float32) * scale
        s_sel = s_sel - slopes[None, :, None, None, None] * d_sel
        s_sel = jnp.where(vdc[..., None], s_sel, -jnp.inf)
        Bq, Hq, C = s_own.shape[:3]
        s_all = jnp.concatenate([s_sel.reshape(Bq, Hq, C, n_sel * MOBA_BLOCK), s_own], axis=-1)
        p = jax.nn.softmax(s_all, axis=-1)
        p_sel = p[..., :n_sel * MOBA_BLOCK].reshape(Bq, Hq, C, n_sel, MOBA_BLOCK)
        p_own = p[..., n_sel * MOBA_BLOCK:]
        return (jnp.einsum('bhqjk,bhqjkd->bhqd', p_sel.astype(v_sel.dtype), v_sel)
                + jnp.einsum('bhqk,bhkd->bhqd', p_own.astype(v_own.dtype), v_own))

    o = lax.map(chunk, (qc_all, idx_all, val_all, starts))
    return o.transpose(1, 0, 3, 2, 4).reshape(B, S, H * dh)


def setup_inputs(seed: int = 0) -> dict:
    key = jax.random.key(seed)
    ks = jax.random.split(key, 20)
    f32 = jnp.float32
    L, D = DEPTH, D_MODEL

    def nrm(k, shape, fan_in):
        return jax.random.normal(k, shape, f32) * (fan_in ** -0.5)

    def gain(k, shape):
        return 1.0 + 0.02 * jax.random.normal(k, shape, f32)

    return {
        "x": jax.random.normal(ks[0], (BATCH, SEQ, D), f32),
        "g_ffn1": gain(ks[1], (L, D)),
        "w_ffn1_gu": nrm(ks[2], (L, D, 2 * D_FF), D),
        "w_ffn1_down": nrm(ks[3], (L, D_FF, D), D_FF),
        "g_mix": gain(ks[4], (L, D)),
        "w_in": nrm(ks[5], (L, D, IN_COLS), D),
        "lam_q1": 0.1 * jax.random.normal(ks[6], (L, DIFF_HEAD_DIM), f32),
        "lam_k1": 0.1 * jax.random.normal(ks[7], (L, DIFF_HEAD_DIM), f32),
        "lam_q2": 0.1 * jax.random.normal(ks[8], (L, DIFF_HEAD_DIM), f32),
        "lam_k2": 0.1 * jax.random.normal(ks[9], (L, DIFF_HEAD_DIM), f32),
        "g_subln": gain(ks[10], (L, 2 * DIFF_HEAD_DIM)),
        "p_a": nrm(ks[11], (L, DIFF_WIDTH, D), DIFF_WIDTH),
        "p_b": nrm(ks[12], (L, MOBA_WIDTH, D), MOBA_WIDTH),
        "w_o": nrm(ks[13], (L, D, D), D),
        "g_ffn2": gain(ks[14], (L, D)),
        "w_ffn2_gu": nrm(ks[15], (L, D, 2 * D_FF), D),
        "w_ffn2_down": nrm(ks[16], (L, D_FF, D), D_FF),
        "g_final": gain(ks[17], (D,)),
    }


def reference(x, g_ffn1, w_ffn1_gu, w_ffn1_down, g_mix, w_in, lam_q1, lam_k1, lam_q2, lam_k2,
              g_subln, p_a, p_b, w_o, g_ffn2, w_ffn2_gu, w_ffn2_down, g_final):
    B, S, D = x.shape
    c0 = 0
    offs = []
    for w in (DIFF_WIDTH, DIFF_WIDTH, DIFF_WIDTH, MOBA_WIDTH, MOBA_WIDTH, MOBA_WIDTH, D_MODEL):
        c0 += w
        offs.append(c0)
    for l in range(DEPTH):
        x = x + 0.5 * swiglu(rmsnorm(x, g_ffn1[l]), w_ffn1_gu[l], w_ffn1_down[l])
        h = rmsnorm(x, g_mix[l])
        proj = h @ w_in[l]
        qa, ka, va, qb, kb, vb, gate_a, gate_b = jnp.split(proj, offs, axis=-1)
        qa = qa.reshape(B, S, N_HEADS_DIFF, 2, DIFF_HEAD_DIM)
        ka = ka.reshape(B, S, N_HEADS_DIFF, 2, DIFF_HEAD_DIM)
        va = va.reshape(B, S, N_HEADS_DIFF, 2 * DIFF_HEAD_DIM)
        lam_init = 0.8 - 0.6 * math.exp(-0.3 * l)
        lam = (jnp.exp(jnp.sum(lam_q1[l].astype(jnp.float32) * lam_k1[l].astype(jnp.float32)))
               - jnp.exp(jnp.sum(lam_q2[l].astype(jnp.float32) * lam_k2[l].astype(jnp.float32)))
               + lam_init)
        o_a = diff_attention(qa, ka, va, lam, g_subln[l], lam_init)
        o_b = moba_attention(qb.reshape(B, S, N_HEADS_MOBA, MOBA_HEAD_DIM),
                             kb.reshape(B, S, N_HEADS_MOBA, MOBA_HEAD_DIM),
                             vb.reshape(B, S, N_HEADS_MOBA, MOBA_HEAD_DIM))
        merged = jax.nn.sigmoid(gate_a) * (o_a @ p_a[l]) + jax.nn.sigmoid(gate_b) * (o_b @ p_b[l])
        x = x + merged @ w_o[l]
        x = x + 0.5 * swiglu(rmsnorm(x, g_ffn2[l]), w_ffn2_gu[l], w_ffn2_down[l])
    return rmsnorm(x, g_final)
```

```python
import contextlib
import os
import numpy as np
import ml_dtypes
import concourse.bass as bass
import concourse.mybir as mybir
from concourse.bass_utils import run_bass_kernel_spmd

F32 = mybir.dt.float32
BF16 = mybir.dt.bfloat16
AF = mybir.ActivationFunctionType
ALU = mybir.AluOpType
AX = mybir.AxisListType

D = 2048
DC = 16
NT = 1024
FF = 5632
FCH = 44
EPS = 1e-6
NEG = -30000.0
LAM_INIT = 0.2
NBANK = 8
NSLOT = 4
SLOT_ELEMS = 4096


class Res:
    __slots__ = ("name", "w", "r")

    def __init__(self, name):
        self.name = name
        self.w = None
        self.r = {}


class Ins:
    __slots__ = ("eng", "fn", "deps", "signal", "val", "dma", "sem", "idx", "prewait")

    def __init__(self, eng, fn, dma):
        self.eng = eng
        self.fn = fn
        self.deps = {}
        self.signal = False
        self.val = None
        self.dma = dma
        self.sem = None
        self.idx = None
        self.prewait = None


class Prog:
    ENGS = ("pe", "act", "dve", "pool", "sp")
    BUCKET = 240

    def __init__(self):
        self.q = {e: [] for e in self.ENGS}
        self.n = 0

    def add(self, eng, fn, reads=(), writes=(), dma=False, after=()):
        ins = Ins(eng, fn, dma)
        ins.idx = self.n
        self.n += 1

        def dep(p):
            if p is None:
                return
            if p.eng == "pe" and eng == "pe" and not p.dma:
                return
            key = ("dma", p.idx) if p.dma else p.eng
            old = ins.deps.get(key)
            if old is None or old.idx < p.idx:
                ins.deps[key] = p

        for r in reads:
            dep(r.w)
        for w in tuple(writes) + tuple(after):
            dep(w.w)
            for rr in w.r.values():
                dep(rr)
        for r in reads:
            key = ("dma", ins.idx) if dma else eng
            r.r[key] = ins
        for w in writes:
            w.w = ins
            w.r = {}
        self.q[eng].append(ins)
        return ins

    def finalize(self, prog_sems, dma_sems):
        for e in self.ENGS:
            for ins in self.q[e]:
                for p in ins.deps.values():
                    p.signal = True
        cnt = {e: 0 for e in self.ENGS}
        dma_tot = [0] * len(dma_sems)
        dma_rr = 0
        for e in self.ENGS:
            pass
        allins = sorted((i for e in self.ENGS for i in self.q[e]), key=lambda i: i.idx)
        half = len(dma_sems) // 2
        rr = {"sp": 0, "pool": 0}
        for ins in allins:
            if ins.dma:
                base = 0 if ins.eng == "sp" else half
                s = base + rr[ins.eng] % half
                rr[ins.eng] += 1
                ins.sem = s
                ins.prewait = dma_tot[s]
                dma_tot[s] += 16
                ins.val = dma_tot[s]
            elif ins.signal:
                n = cnt[ins.eng]
                cnt[ins.eng] += 1
                ins.sem = n // self.BUCKET
                ins.val = n % self.BUCKET + 1
        self.nbuckets = {e: (cnt[e] + self.BUCKET - 1) // self.BUCKET for e in self.ENGS}
        self.dma_sems = dma_sems

    def emit(self, eng_name, eng):
        waited = {}

        def wait(sem, val):
            if val <= 0:
                return
            if waited.get(sem.name, 0) >= val:
                return
            waited[sem.name] = val
            eng.wait_ge(sem, val)

        for ins in self.q[eng_name]:
            for p in ins.deps.values():
                if p.dma:
                    wait(self.dma_sems[p.sem], p.val)
                else:
                    wait(self.prog_sems[p.eng][p.sem], p.val)
            if ins.dma:
                wait(self.dma_sems[ins.sem], ins.prewait)
                ins.fn(eng).then_inc(self.dma_sems[ins.sem], 16)
            else:
                r = ins.fn(eng)
                if ins.signal:
                    r.then_inc(self.prog_sems[ins.eng][ins.sem], 1)


def build(debug=False, lvl=9, fast=False, only=()):
    nc = bass.Bass("TRN2", target_bir_lowering=False)
    P = Prog()

    def din(name, shape, dt=F32):
        if fast and name.startswith(("w_", "p_")):
            return None
        return nc.dram_tensor(name, list(shape), dt, kind="ExternalInput").ap()

    xT_d = din("xT", [D, NT])
    xcT_d = din("xcT", [D, NT])
    wgu1 = din("w_gu1", [D, 2 * FF])
    wdn1 = din("w_dn1", [FF, D])
    win = din("w_in", [D, 10240])
    pa_d = din("p_a", [1024, D])
    pb_d = din("p_b", [1024, D])
    wo_d = din("w_o", [D, D])
    wgu2 = din("w_gu2", [D, 2 * FF])
    wdn2 = din("w_dn2", [FF, D])
    cst_d = din("cst", [128, 384])
    kaugA_d = din("kaugA", [8, 5, 2048], BF16)
    qaugA_d = din("qaugA", [5, NT], BF16)
    kaugB_d = din("kaugB", [8, 69, 2048], BF16)
    qaugB_d = din("qaugB", [5, NT], BF16)
    ptab_d = din("ptab", [128, 3, 64])
    tri_d = din("tri", [128, 128], BF16)
    ident_d = din("ident", [128, 128])
    out_d = nc.dram_tensor("outT", [D, NT], F32, kind="ExternalOutput").ap()

    KA_T = nc.dram_tensor("KA_T", [1024, 2048], BF16).ap()
    KB_T = nc.dram_tensor("KB_T", [1024, 2048], BF16).ap()
    VA_s = nc.dram_tensor("VA_s", [2048, 1024], BF16).ap()
    VB_s = nc.dram_tensor("VB_s", [2048, 1024], BF16).ap()
    QA_T = nc.dram_tensor("QA_T", [1024, NT], BF16).ap()
    QB_T = nc.dram_tensor("QB_T", [1024, NT], BF16).ap()
    dbg = {}
    if debug:
        for nm, shp in (("d_small0", [128, 16]), ("d_small1", [128, 16]), ("d_small2", [128, 16]), ("d_x1T", [D, NT]), ("d_oA0", [1024, NT]), ("d_oA", [1024, NT]), ("d_oB", [1024, NT]), ("d_x2T", [D, NT]),
                        ("d_sel", [64, NT])):
            dbg[nm] = nc.dram_tensor(nm, shp, F32, kind="ExternalOutput").ap()

    es = contextlib.ExitStack()
    with es:
        def sb(name, shape, dt):
            return es.enter_context(nc.sbuf_tensor(name, list(shape), dt))

        xT = sb("xT_sb", [128, DC, NT], F32)
        hT = sb("hT_sb", [128, DC * NT], BF16)
        R3 = sb("r3_sb", [128, 20 * 1024], BF16)
        ring = [sb(f"ring{i}", [128, SLOT_ELEMS], BF16) for i in range(NSLOT)]
        sq = [sb(f"sq{i}", [128, NT], BF16) for i in range(2)]
        rstd = sb("rstd", [128, NT], F32)
        tmpb = [sb(f"tmpb{i}", [128, 512], BF16) for i in range(6)]
        tmpf = [sb(f"tmpf{i}", [128, 512], F32) for i in range(3)]
        stage = [sb(f"stage{i}", [128, NT], BF16) for i in range(2)]
        cst = sb("cst_sb", [128, 384], F32)
        small = sb("small", [128, 16], F32)
        epsc = sb("epsc", [128, 1], F32)
        onesb = sb("onesb", [128, 128], BF16)
        onesD = sb("onesD", [128, 128], BF16)
        ones128 = sb("ones128", [128, 128], BF16)
        tri = sb("tri_sb", [128, 128], BF16)
        ident = sb("ident_sb", [128, 128], F32)
        ptab = sb("ptab_sb", [128, 3, 64], F32)
        ksum = sb("ksum", [128, 64], F32)
        ksh = sb("ksh", [128, 64], BF16)
        ksl = sb("ksl", [128, 64], BF16)
        gsc = sb("gsc", [128, 64], F32)
        gcmp = sb("gcmp", [128, 512], F32)
        grank = sb("grank", [128, 64], F32)
        qaugB = sb("qaugB_sb", [69, NT], BF16)
        banks = [es.enter_context(nc.psum_tensor(f"bank{i}", [128, 512], F32)) for i in range(NBANK)]

        dma_sems = [es.enter_context(nc.semaphore(f"dma{i}")) for i in range(24)]

        R_x = [Res(f"x{c}") for c in range(DC)]
        R_h = [Res(f"h{c}") for c in range(DC)]
        R_r3 = [Res(f"r3_{j}") for j in range(20)]
        R_ringp = [[Res(f"ring{i}a"), Res(f"ring{i}b")] for i in range(NSLOT)]
        R_sq = [Res("sq0"), Res("sq1")]
        R_rstd = Res("rstd")
        R_tmpb = [Res(f"tmpb{i}") for i in range(6)]
        R_tmpf = [Res(f"tmpf{i}") for i in range(3)]
        R_stage = [Res("stage0"), Res("stage1")]
        R_bank = [Res(f"bank{i}") for i in range(NBANK)]
        R_cst = Res("cst")
        R_small = Res("small")
        R_misc = Res("misc")
        R_g = Res("gtmp")
        R_qaugB = Res("qaugB")
        R_dram = {}

        def rd(name):
            if name not in R_dram:
                R_dram[name] = Res(name)
            return R_dram[name]

        st = {"bank": 0, "slot": 0, "tb": 0, "tf": 0, "stg": 0}

        def nbank():
            b = st["bank"]
            st["bank"] = (b + 1) % NBANK
            return b

        def ntb():
            b = st["tb"]
            st["tb"] = (b + 1) % 6
            return b

        def nstage():
            b = st["stg"]
            st["stg"] = (b + 1) % 2
            return b

        def mm(out, lhsT, rhs, start, stop, reads=(), writes=()):
            return P.add("pe", lambda e: e.matmul(out, lhsT, rhs, start=start, stop=stop), reads, writes)

        def act(out, in_, func, reads, writes, bias=None, scale=None):
            kw = {}
            if bias is not None:
                kw["bias"] = bias
            if scale is not None:
                kw["scale"] = scale
            return P.add("act", lambda e: e.activation(out, in_, func, **kw), reads, writes)

        def dve(fn, reads, writes, after=()):
            return P.add("dve", fn, reads, writes, after=after)

        def dma(q, out, in_, reads, writes, after=()):
            return P.add(q, lambda e: e.dma_start(out=out, in_=in_), reads, writes, dma=True, after=after)

        def wload(parts):
            s = st["slot"]
            st["slot"] = (s + 1) % NSLOT
            for i, (views, src) in enumerate(parts):
                dma("pool", views(ring[s]), src, (), (R_ringp[s][i],))
            return s

        def ring_view(s, K, ncols):
            return ring[s][:, 0:K * ncols].rearrange("p (k n) -> p k n", n=ncols)

        def hTv(c, lo, hi):
            return hT[:, c * NT + lo:c * NT + hi]

        dma("sp", cst[:], cst_d, (), (R_cst,))
        dma("sp", tri[:], tri_d, (), (R_misc,))
        dma("sp", ident[:], ident_d, (), (R_misc,))
        dma("sp", ptab[:], ptab_d, (), (R_misc,))
        dve(lambda e: e.memset(onesb[:], 1.0), (), (R_misc,))
        dve(lambda e: e.memset(epsc[:], EPS), (), (R_misc,))
        dve(lambda e: e.memset(onesD[:], 1.0 / D), (), (R_misc,))
        dve(lambda e: e.memset(ones128[:], 1.0 / 128.0), (), (R_misc,))
        dve(lambda e: e.tensor_tensor(out=gcmp[:, 0:64], in0=cst[:, 128:192], in1=cst[:, 192:256], op=ALU.mult), (R_cst,), (R_g,))
        dve(lambda e: e.tensor_reduce(out=small[:, 0:1], in_=gcmp[:, 0:64], axis=AX.X, op=ALU.add), (R_g,), (R_small,))
        dve(lambda e: e.tensor_tensor(out=gcmp[:, 64:128], in0=cst[:, 256:320], in1=cst[:, 320:384], op=ALU.mult), (R_cst,), (R_g,))
        dve(lambda e: e.tensor_reduce(out=small[:, 1:2], in_=gcmp[:, 64:128], axis=AX.X, op=ALU.add), (R_g,), (R_small,))
        act(small[:, 2:4], small[:, 0:2], AF.Exp, (R_small,), (R_small,))
        dve(lambda e: e.tensor_tensor(out=small[:, 4:5], in0=small[:, 3:4], in1=small[:, 2:3], op=ALU.subtract), (R_small,), (R_small,))
        dve(lambda e: e.tensor_scalar(out=small[:, 4:5], in0=small[:, 4:5], scalar1=-LAM_INIT, scalar2=None, op0=ALU.add), (R_small,), (R_small,))
        dve(lambda e: e.tensor_scalar(out=small[:, 5:6], in0=cst[:, 64:65], scalar1=1.0 - LAM_INIT, scalar2=None, op0=ALU.mult), (R_cst, R_small), (R_small,))

        dump_small = lambda i: dma("sp", dbg[f"d_small{i}"], small[:], (R_small,), (rd(f"dbgsmall{i}"),)) if debug else None
        dump_small(0)
        def load_x(src):
            for c4 in range(4):
                dma("sp", xT[:, c4 * 4:(c4 + 1) * 4, :], src.rearrange("(c p) t -> p c t", p=128)[:, c4 * 4:(c4 + 1) * 4, :],
                    (), tuple(R_x[c4 * 4:(c4 + 1) * 4]))

        def norm_stats():
            bs = [nbank(), nbank()]
            for c in range(DC):
                i = c % 2
                act(sq[i][:], xT[:, c, :], AF.Square, (R_x[c],), (R_sq[i],))
                for th in range(2):
                    mm(banks[bs[th]][:], onesD[:], sq[i][:, th * 512:(th + 1) * 512], c == 0, c == DC - 1,
                       (R_sq[i], R_misc), (R_bank[bs[th]],) if c in (0, DC - 1) else ())
            for th in range(2):
                b = bs[th]
                act(rstd[:, th * 512:(th + 1) * 512], banks[b][:], AF.Sqrt, (R_bank[b], R_misc), (R_rstd,), bias=epsc[:, 0:1])
                dve(lambda e, th=th: e.reciprocal(out=rstd[:, th * 512:(th + 1) * 512], in_=rstd[:, th * 512:(th + 1) * 512]), (R_rstd,), (R_rstd,))

        def norm_to_h(gcol0):
            norm_stats()
            for c in range(DC):
                dve(lambda e, c=c: e.scalar_tensor_tensor(out=hTv(c, 0, NT), in0=xT[:, c, :], scalar=cst[:, gcol0 + c:gcol0 + c + 1],
                                                        in1=rstd[:], op0=ALU.mult, op1=ALU.mult),
                    (R_x[c], R_rstd, R_cst), (R_h[c],))

        def ffn(wgu, wdn):
            hid = lambda j, lo, hi: R3[:, j * NT + lo:j * NT + hi]
            for qtr in range(4):
                for j in range(11):
                    f = qtr * 11 + j
                    wv = wgu.rearrange("(k p) n -> p k n", p=128)
                    s = wload([
                        (lambda t: ring_view_t(t, 16, 256)[:, :, 0:128], wv[:, :, f * 128:(f + 1) * 128]),
                        (lambda t: ring_view_t(t, 16, 256)[:, :, 128:256], wv[:, :, FF + f * 128:FF + (f + 1) * 128]),
                    ])
                    w = ring_view(s, 16, 256)
                    bg = [nbank(), nbank()]
                    bu = [nbank(), nbank()]
                    for k in range(DC):
                        for th in range(2):
                            fl = (R_bank[bg[th]],) if k in (0, DC - 1) else ()
                            mm(banks[bg[th]][:], w[:, k, 0:128], hTv(k, th * 512, (th + 1) * 512), k == 0, k == DC - 1,
                               (R_ringp[s][0], R_ringp[s][1], R_h[k]), fl)
                        for th in range(2):
                            fl = (R_bank[bu[th]],) if k in (0, DC - 1) else ()
                            mm(banks[bu[th]][:], w[:, k, 128:256], hTv(k, th * 512, (th + 1) * 512), k == 0, k == DC - 1,
                               (R_ringp[s][0], R_ringp[s][1], R_h[k]), fl)
                    for th in range(2):
                        tb = ntb()
                        act(tmpb[tb][:], banks[bg[th]][:], AF.Silu, (R_bank[bg[th]],), (R_tmpb[tb],))
                        dve(lambda e, tb=tb, th=th, j=j, b=bu[th]: e.tensor_tensor(out=hid(j, th * 512, (th + 1) * 512), in0=tmpb[tb][:],
                                                                                 in1=banks[b][:], op=ALU.mult),
                            (R_tmpb[tb], R_bank[bu[th]]), (R_r3[j],))
                for dt in range(8):
                    wv = wdn[qtr * 11 * 128:(qtr + 1) * 11 * 128, dt * 256:(dt + 1) * 256].rearrange("(j p) n -> p j n", p=128)
                    s = wload([(lambda t: ring_view_t(t, 11, 256), wv)])
                    w = ring_view(s, 11, 256)
                    for cc in range(2):
                        c = dt * 2 + cc
                        for th in range(2):
                            b = nbank()
                            for j in range(11):
                                fl = (R_bank[b],) if j in (0, 10) else ()
                                mm(banks[b][:], w[:, j, cc * 128:(cc + 1) * 128], hid(j, th * 512, (th + 1) * 512), j == 0, j == 10,
                                   (R_ringp[s][0], R_ringp[s][1], R_r3[j]), fl)
                            dve(lambda e, b=b, c=c, th=th: e.scalar_tensor_tensor(out=xT[:, c, th * 512:(th + 1) * 512], in0=banks[b][:], scalar=0.5,
                                                                                in1=xT[:, c, th * 512:(th + 1) * 512], op0=ALU.mult, op1=ALU.add),
                                (R_bank[b], R_x[c]), (R_x[c],))

        def ring_view_t(t, K, ncols):
            return t[:, 0:K * ncols].rearrange("p (k n) -> p k n", n=ncols)

        def proj_T(col0, dst, tok0, scale, key):
            wv = win.rearrange("(k p) n -> p k n", p=128)
            for t in range(4):
                s = wload([(lambda tt: ring_view_t(tt, 16, 256), wv[:, :, col0 + t * 256:col0 + (t + 1) * 256])])
                w = ring_view(s, 16, 256)
                for cc in range(2):
                    ch = t * 2 + cc
                    bs = [nbank(), nbank()]
                    for k in range(DC):
                        for th in range(2):
                            fl = (R_bank[bs[th]],) if k in (0, DC - 1) else ()
                            mm(banks[bs[th]][:], w[:, k, cc * 128:(cc + 1) * 128], hTv(k, th * 512, (th + 1) * 512), k == 0, k == DC - 1,
                               (R_ringp[s][0], R_ringp[s][1], R_h[k]), fl)
                    sg = nstage()
                    for th in range(2):
                        if th == 0:
                            act(stage[sg][:, 0:512], banks[bs[0]][:], AF.Identity, (R_bank[bs[0]],), (R_stage[sg],), scale=scale)
                        else:
                            dve(lambda e, sg=sg, b=bs[1]: e.tensor_scalar(out=stage[sg][:, 512:1024], in0=banks[b][:], scalar1=scale, scalar2=None, op0=ALU.mult),
                                (R_bank[bs[1]],), (R_stage[sg],))
                    dma("sp", dst[ch * 128:(ch + 1) * 128, tok0:tok0 + NT], stage[sg][:], (R_stage[sg],), (rd(f"{key}{ch}"),))

        def proj_V(col0, dst, tok0, key):
            wv = win.rearrange("(k p) n -> p k n", p=128)
            for t in range(4):
                s = wload([(lambda tt: ring_view_t(tt, 16, 256), wv[:, :, col0 + t * 256:col0 + (t + 1) * 256])])
                w = ring_view(s, 16, 256)
                for g4 in range(2):
                    sg = nstage()
                    for ti in range(4):
                        tt = g4 * 4 + ti
                        b = nbank()
                        for k in range(DC):
                            fl = (R_bank[b],) if k in (0, DC - 1) else ()
                            mm(banks[b][:, 0:256], hTv(k, tt * 128, (tt + 1) * 128), w[:, k, :], k == 0, k == DC - 1,
                               (R_ringp[s][0], R_ringp[s][1], R_h[k]), fl)
                        if ti % 2 == 0:
                            act(stage[sg][:, ti * 256:(ti + 1) * 256], banks[b][:, 0:256], AF.Copy, (R_bank[b],), (R_stage[sg],))
                        else:
                            dve(lambda e, sg=sg, b=b, ti=ti: e.tensor_copy(out=stage[sg][:, ti * 256:(ti + 1) * 256], in_=banks[b][:, 0:256]),
                                (R_bank[b],), (R_stage[sg],))
                    dview = dst[tok0 + g4 * 512:tok0 + (g4 + 1) * 512, t * 256:(t + 1) * 256].rearrange("(i p) n -> p i n", p=128)
                    dma("sp", dview, stage[sg][:].rearrange("p (i n) -> p i n", n=256), (R_stage[sg],), (rd(f"{key}{t}_{g4}_{tok0}"),))

        def bcast_mid(ap2, n):
            return ap2.unsqueeze(1).broadcast_to([128, n, ap2.shape[1]])

        def bcast_g_other(g):
            return g[:].rearrange("p (h s) -> p h s", s=8).unsqueeze(2).broadcast_to([128, 8, 8, 8])

        def bcast_g_self(g):
            return g[:].rearrange("p (h s) -> p h s", s=8).unsqueeze(3).broadcast_to([128, 8, 8, 8])

        def dump(name, sb_ap, reads):
            if debug:
                dma("sp", dbg[name], sb_ap, reads, (rd("dbg_" + name),))

        if fast:
            ffn = lambda *a, **k: None
            proj_T = lambda *a, **k: None
            proj_V = lambda *a, **k: None
        if lvl >= 2:
            load_x(xcT_d)
            norm_to_h(0)
            ffn(wgu1, wdn1)
            norm_to_h(16)
            proj_T(1024, KA_T, 0, 1.0, "ka0_")
            proj_V(2048, VA_s, 0, "va")
            proj_T(4096, KB_T, 0, 1.0, "kb0_")
            proj_V(5120, VB_s, 0, "vb")
        load_x(xT_d)
        norm_to_h(0)
        ffn(wgu1, wdn1)
        if debug:
            for c in range(DC):
                dma("sp", dbg["d_x1T"][c * 128:(c + 1) * 128, :], xT[:, c, :], (R_x[c],), (rd(f"dbgx1{c}"),))
        if lvl >= 3:
            norm_to_h(16)
            proj_T(0, QA_T, 0, 0.125, "qa_")
            proj_T(1024, KA_T, NT, 1.0, "ka1_")
            proj_V(2048, VA_s, NT, "va")
            proj_T(3072, QB_T, 0, 128.0 ** -0.5, "qb_")
            proj_T(4096, KB_T, NT, 1.0, "kb1_")
            proj_V(5120, VB_s, NT, "vb")

            all_h = tuple(R_h)
            KVQ = hT

            def kvq(off, n):
                return KVQ[:, off:off + n]

            all_dram_k = lambda pre: tuple(rd(f"{pre}0_{ch}") for ch in range(8)) + tuple(rd(f"{pre}1_{ch}") for ch in range(8))
            all_dram_v = lambda pre: tuple(rd(f"{pre}{t}_{g}_{tk}") for t in range(4) for g in range(2) for tk in (0, NT))
            oA = lambda h, lo, hi: R3[:, h * NT + lo:h * NT + hi]
            oB = lambda h, lo, hi: R3[:, (8 + h) * NT + lo:(8 + h) * NT + hi]

            SB = [0, 1, 2]
            OB = [3, 4]
            SMB = [5, 6]
            XB = 7
            sidx = [0]

            def ktiles(qc):
                lst = [(kt, 0, False) for kt in range(8)]
                for j in range(4 * qc + 4):
                    if j < 4 * qc:
                        lst.append((8 + j, 0, False))
                    else:
                        lst.append((8 + j, 128 * (j - 4 * qc), True))
                return lst

            NTB = len(tmpb)
            SBK = [0, 1, 2, 7]

            def run_tiles(tiles, LOOK=2):
                tbs = {}
                n = len(tiles)

                def issue(i):
                    t = tiles[i]
                    b = SBK[sidx[0] % len(SBK)]
                    sidx[0] += 1
                    col0 = t["col0"]
                    t["score"](b)
                    tb = st["tb"]
                    st["tb"] = (tb + 1) % NTB
                    tbs[i] = tb
                    act(tmpb[tb][:, col0:512], banks[b][:, col0:512], AF.Exp, (R_bank[b],), (R_tmpb[tb],))
                    if t["diag"]:
                        dve(lambda e, tb=tb, col0=col0: e.tensor_tensor(out=tmpb[tb][:, col0:col0 + 128], in0=tmpb[tb][:, col0:col0 + 128], in1=tri[:], op=ALU.mult),
                            (R_tmpb[tb], R_misc), (R_tmpb[tb],))

                deferred = []
                st["defer"] = lambda k, fn: deferred.append([k, fn])
                for i in range(min(LOOK, n)):
                    issue(i)
                for i, t in enumerate(tiles):
                    if i + LOOK < n:
                        issue(i + LOOK)
                    for d in [d for d in deferred if d[0] <= 0]:
                        deferred.remove(d)
                        d[1]()
                    for d in deferred:
                        d[0] -= 1
                    tb = tbs[i]
                    col0, ob, smb, first, last = t["col0"], t["ob"], t["smb"], t["first"], t["last"]
                    mm(banks[ob][:, col0:512], t["V"], tmpb[tb][:, col0:512], first, last,
                       (R_tmpb[tb],) + tuple(t["_ld"][t["head"]]), (R_bank[ob],) if (first or last) else ())
                    mm(banks[smb][:, col0:512], onesb[:], tmpb[tb][:, col0:512], first, last,
                       (R_tmpb[tb], R_misc), (R_bank[smb],) if (first or last) else ())
                    for p in t["post"]:
                        p()
                for d in deferred:
                    d[1]()

            R_set = [Res("kvqset0"), Res("kvqset1")]
            if lvl >= 5:
                def bufsA(h):
                    base = (h % 2) * 8192
                    KA = [KVQ[0:69, base + m * 2048:base + (m + 1) * 2048] for m in range(2)]
                    QA = [KVQ[0:69, base + 4096 + m * 1024:base + 4096 + (m + 1) * 1024] for m in range(2)]
                    VAh = KVQ[:, base + 6144:base + 8192]
                    return KA, QA, VAh

                ldA = {}

                def loadA(h):
                    KA, QA, VAh = bufsA(h)
                    rs = R_set[h % 2]
                    aft = (rs,) + (all_h if h < 2 else ())
                    ld = [Res(f"ldA{h}_{i}") for i in range(9)]
                    for m in range(2):
                        r0 = h * 128 + m * 64
                        dma("sp", KA[m][0:64, :], KA_T[r0:r0 + 64, :], all_dram_k("ka"), (ld[4 * m],), aft)
                        dma("sp", KA[m][64:69, :], kaugA_d[h], (), (ld[4 * m + 1],), aft)
                        dma("sp", QA[m][0:64, :], QA_T[r0:r0 + 64, :], tuple(rd(f"qa_{ch}") for ch in range(8)), (ld[4 * m + 2],), aft)
                        dma("sp", QA[m][64:69, :], qaugA_d, (), (ld[4 * m + 3],), aft)
                    dma("sp", VAh.rearrange("p (t d) -> p t d", d=128), VA_s[:, h * 128:(h + 1) * 128].rearrange("(t p) d -> p t d", p=128),
                        all_dram_v("va"), (ld[8],), aft)
                    ldA[h] = (rs,) + tuple(ld)

                def postA0():
                    t0 = tmpf[0]
                    dve(lambda e, t0=t0: e.reciprocal(out=t0[:], in_=banks[SMB[0]][:]), (R_bank[SMB[0]],), (R_tmpf[0],))
                    dve(lambda e, t0=t0: e.tensor_tensor(out=t0[:], in0=banks[OB[0]][:], in1=t0[:], op=ALU.mult), (R_bank[OB[0]], R_tmpf[0]), (R_tmpf[0],))

                def postA1(h, qc):
                    t0, t1 = tmpf[0], tmpf[1]
                    dve(lambda e, t1=t1: e.reciprocal(out=t1[:], in_=banks[SMB[1]][:]), (R_bank[SMB[1]],), (R_tmpf[1],))
                    dve(lambda e, t1=t1: e.tensor_tensor(out=t1[:], in0=banks[OB[1]][:], in1=t1[:], op=ALU.mult), (R_bank[OB[1]], R_tmpf[1]), (R_tmpf[1],))
                    dve(lambda e, t0=t0, t1=t1: e.scalar_tensor_tensor(out=t0[:], in0=t1[:], scalar=small[:, 4:5], in1=t0[:], op0=ALU.mult, op1=ALU.add),
                        (R_tmpf[0], R_tmpf[1], R_small), (R_tmpf[0],))
                    tb = st["tb"]
                    st["tb"] = (tb + 1) % NTB
                    dve(lambda e, t0=t0, tb=tb: e.tensor_tensor(out=tmpb[tb][:], in0=t0[:], in1=t0[:], op=ALU.mult), (R_tmpf[0],), (R_tmpb[tb],))
                    sqb = tmpf[2]

                    def stage2(h=h, qc=qc, t0=t0, t1=t1, tb=tb):
                        xb = SBK[sidx[0] % len(SBK)]
                        sidx[0] += 1
                        mm(banks[xb][:], ones128[:], tmpb[tb][:], True, True, (R_tmpb[tb], R_misc), (R_bank[xb],))
                        act(t1[:], banks[xb][:], AF.Ln, (R_bank[xb], R_misc), (R_tmpf[1],), bias=epsc[:, 0:1])
                        act(t1[:], t1[:], AF.Exp, (R_tmpf[1],), (R_tmpf[1],), scale=-0.5)
                        dve(lambda e, h=h, qc=qc, t0=t0, t1=t1: e.scalar_tensor_tensor(out=oA(h, qc * 512, (qc + 1) * 512), in0=t0[:], scalar=small[:, 5:6], in1=t1[:],
                                                                                   op0=ALU.mult, op1=ALU.mult),
                            (R_tmpf[0], R_tmpf[1], R_small), (R_r3[h],))
                    st["defer"](4, stage2)

                loadA(0)
                loadA(1)
                tilesA = []
                for h in range(8):
                    KA, QA, VAh = bufsA(h)
                    for qc in range(2):
                        for m in range(2):
                            kl = ktiles(qc)
                            for i, (kt, col0, diag) in enumerate(kl):
                                def score(b, h=h, m=m, qc=qc, kt=kt, col0=col0, KA=KA, QA=QA):
                                    mm(banks[b][:, col0:512], KA[m][:, kt * 128:(kt + 1) * 128], QA[m][:, qc * 512 + col0:(qc + 1) * 512], True, True,
                                       ldA[h], (R_bank[b],))
                                post = []
                                if i == len(kl) - 1:
                                    if m == 0:
                                        post.append(postA0)
                                    else:
                                        post.append(lambda h=h, qc=qc: postA1(h, qc))
                                        if qc == 1 and h + 2 < 8:
                                            post.append(lambda h=h: loadA(h + 2))
                                tilesA.append(dict(score=score, col0=col0, diag=diag, V=VAh[:, kt * 128:(kt + 1) * 128], ob=OB[m], smb=SMB[m],
                                                   first=i == 0, last=i == len(kl) - 1, reads_v=None, post=post, head=h))
                for t in tilesA:
                    t["_ld"] = ldA
                run_tiles(tilesA)


            dump_small(1)
            if debug and lvl >= 5:
                for h in range(8):
                    for qc in range(2):
                        dve(lambda e, h=h, qc=qc: e.tensor_copy(out=tmpf[2][:], in_=oA(h, qc * 512, (qc + 1) * 512)), (R_r3[h],), (R_tmpf[2],))
                        dma("sp", dbg["d_oA0"][h * 128:(h + 1) * 128, qc * 512:(qc + 1) * 512], tmpf[2][:], (R_tmpf[2],), (rd(f"dbgoA0{h}{qc}"),))
            if lvl >= 6:
                GS = int(os.environ.get("GSTEP", "99"))
                R_qball = Res("qball")
                rsq = R_qball
                rsk = [Res("kbtmp0"), Res("kbtmp1")]
                QBall = KVQ[:, 0:8192]
                dma("sp", QBall.rearrange("p (h t) -> p h t", t=NT), QB_T.rearrange("(h p) t -> p h t", p=128),
                    tuple(rd(f"qb_{ch}") for ch in range(8)), (R_qball,), (R_set[0], R_set[1]))
                dma("sp", qaugB[64:69, :], qaugB_d, (), (R_qaugB,))
                for h in range(8 if GS >= 2 else 0):
                    kb = KVQ[:, 8192 + (h % 2) * 2048:8192 + (h % 2 + 1) * 2048]
                    dma("sp", kb, KB_T[h * 128:(h + 1) * 128, :], all_dram_k("kb"), (rsk[h % 2],), (R_set[1],))
                    dve(lambda e, kb=kb, h=h: e.tensor_reduce(out=ksum[:, h * 8:(h + 1) * 8], in_=kb.rearrange("p (s t) -> p s t", t=256), axis=AX.X, op=ALU.add),
                        (rsk[h % 2], R_set[1]), (R_g,))
                dve(lambda e: e.tensor_copy(out=ksh[:], in_=ksum[:]), (R_g,), (R_g,))
                dve(lambda e: e.tensor_tensor(out=gsc[:], in0=ksum[:], in1=ksh[:], op=ALU.subtract), (R_g,), (R_g,))
                dve(lambda e: e.tensor_copy(out=ksl[:], in_=gsc[:]), (R_g,), (R_g,))
                gb = nbank()
                for qt in range(8):
                    for h in range(8):
                        fl = (R_bank[gb],) if (qt, h) in ((0, 0), (7, 7)) else ()
                        o_ = banks[gb][:, qt * 64 + h * 8:qt * 64 + (h + 1) * 8]
                        q_ = QBall[:, h * NT + qt * 128:h * NT + (qt + 1) * 128]
                        mm(o_, q_, ksh[:, h * 8:(h + 1) * 8], True, False, (rsq, R_g, R_set[0], R_set[1]), fl)
                        mm(o_, q_, ksl[:, h * 8:(h + 1) * 8], False, True, (rsq, R_g, R_set[0], R_set[1]), fl)
                scr = tuple(R_r3[8:18])
                gcmp_all = R3[:, 8 * 1024:16 * 1024].bitcast(F32)
                gall = R3[:, 16 * 1024:17 * 1024].bitcast(F32)
                grk = R3[:, 17 * 1024:18 * 1024].bitcast(F32)
                pt4 = lambda w: ptab[:, w, :].rearrange("p (q s) -> p q s", s=8).unsqueeze(2).broadcast_to([128, 8, 8, 8])
                v4 = lambda ap: ap.rearrange("p (q h s) -> p q h s", h=8, s=8)
                dve(lambda e: e.tensor_tensor(out=v4(gall), in0=v4(banks[gb][:]), in1=pt4(0), op=ALU.add), (R_bank[gb], R_misc), scr)
                a3 = lambda ap: ap.rearrange("p (a s) -> p a s", s=8)
                dve(lambda e: e.tensor_tensor(out=gcmp_all.rearrange("p (a s t) -> p a s t", s=8, t=8),
                                              in0=a3(gall).unsqueeze(2).broadcast_to([128, 64, 8, 8]),
                                              in1=a3(gall).unsqueeze(3).broadcast_to([128, 64, 8, 8]), op=ALU.is_gt), scr, scr)
                dve(lambda e: e.tensor_reduce(out=grk, in_=gcmp_all.rearrange("p (a t) -> p a t", t=8), axis=AX.X, op=ALU.add), scr, scr)
                dve(lambda e: e.tensor_single_scalar(out=grk, in_=grk, scalar=2.5, op=ALU.is_lt), scr, scr)
                dve(lambda e: e.tensor_tensor(out=v4(grk), in0=v4(grk), in1=pt4(1), op=ALU.mult), scr + (R_misc,), scr)
                dve(lambda e: e.tensor_tensor(out=v4(grk), in0=v4(grk), in1=pt4(2), op=ALU.add), scr + (R_misc,), scr)
                dve(lambda e: e.tensor_scalar(out=grk, in0=grk, scalar1=-1.0, scalar2=-NEG, op0=ALU.add, op1=ALU.mult), scr, scr)
                for half in range(2):
                    b2 = nbank()
                    for j in range(4):
                        qt = half * 4 + j
                        P.add("pe", lambda e, b2=b2, j=j, qt=qt: e.transpose(banks[b2][0:64, j * 128:(j + 1) * 128], grk[:, qt * 64:(qt + 1) * 64], ident[:]),
                              scr + (R_misc,), (R_bank[b2],) if j in (0, 3) else ())
                    act(qaugB[0:64, half * 512:(half + 1) * 512], banks[b2][0:64, :], AF.Copy, (R_bank[b2],), (R_qaugB,))

            if lvl >= 7:
                def bufsB(h):
                    base = (h % 2) * 8192
                    return (KVQ[:, base:base + 2048], KVQ[0:69, base + 2048:base + 4096], KVQ[:, base + 4096:base + 5120], KVQ[:, base + 5120:base + 7168])

                ldB = {}

                def loadB(h):
                    KBh, KAUG, QBh, VBh = bufsB(h)
                    rs = R_set[h % 2]
                    ld = [Res(f"ldB{h}_{i}") for i in range(4)]
                    dma("sp", KBh, KB_T[h * 128:(h + 1) * 128, :], all_dram_k("kb"), (ld[0],), (rs,))
                    dma("sp", KAUG, kaugB_d[h], (), (ld[1],), (rs,))
                    dma("sp", QBh, QB_T[h * 128:(h + 1) * 128, :], tuple(rd(f"qb_{ch}") for ch in range(8)), (ld[2],), (rs,))
                    dma("sp", VBh.rearrange("p (t d) -> p t d", d=128), VB_s[:, h * 128:(h + 1) * 128].rearrange("(t p) d -> p t d", p=128),
                        all_dram_v("vb"), (ld[3],), (rs,))
                    ldB[h] = (rs,) + tuple(ld)

                def postB(h, qc):
                    t0 = tmpf[qc]
                    dve(lambda e, t0=t0, qc=qc: e.reciprocal(out=t0[:], in_=banks[SMB[qc]][:]), (R_bank[SMB[qc]],), (R_tmpf[qc],))
                    dve(lambda e, t0=t0, qc=qc, h=h: e.tensor_tensor(out=oB(h, qc * 512, (qc + 1) * 512), in0=banks[OB[qc]][:], in1=t0[:], op=ALU.mult),
                        (R_bank[OB[qc]], R_tmpf[qc]), (R_r3[8 + h],))

                loadB(0)
                loadB(1)
                tilesB = []
                for h in range(8):
                    KBh, KAUG, QBh, VBh = bufsB(h)
                    for qc in range(2):
                        kl = ktiles(qc)
                        for i, (kt, col0, diag) in enumerate(kl):
                            def score(b, h=h, qc=qc, kt=kt, col0=col0, KBh=KBh, KAUG=KAUG, QBh=QBh):
                                mm(banks[b][:, col0:512], KBh[:, kt * 128:(kt + 1) * 128], QBh[:, qc * 512 + col0:(qc + 1) * 512], True, False,
                                   ldB[h], (R_bank[b],))
                                mm(banks[b][:, col0:512], KAUG[:, kt * 128:(kt + 1) * 128], qaugB[:, qc * 512 + col0:(qc + 1) * 512], False, True,
                                   ldB[h] + (R_qaugB,), (R_bank[b],))
                            post = []
                            if i == len(kl) - 1:
                                post.append(lambda h=h, qc=qc: postB(h, qc))
                                if qc == 1 and h + 2 < 8:
                                    post.append(lambda h=h: loadB(h + 2))
                            tilesB.append(dict(score=score, col0=col0, diag=diag, V=VBh[:, kt * 128:(kt + 1) * 128], ob=OB[qc], smb=SMB[qc],
                                               first=i == 0, last=i == len(kl) - 1, reads_v=(), post=post, head=h, _ld=ldB))
                run_tiles(tilesB)

            if debug:
                for h in range(8):
                    for nm, f, off in (("d_oA", oA, 0), ("d_oB", oB, 8)):
                        for qc in range(2):
                            dve(lambda e, f=f, h=h, qc=qc: e.tensor_copy(out=tmpf[2][:], in_=f(h, qc * 512, (qc + 1) * 512)), (R_r3[off + h],), (R_tmpf[2],))
                            dma("sp", dbg[nm][h * 128:(h + 1) * 128, qc * 512:(qc + 1) * 512], tmpf[2][:], (R_tmpf[2],), (rd(f"dbg{nm}{h}{qc}"),))

            if lvl >= 8:
                for c in range(DC):
                    R_h[c].w = None
                    R_h[c].r = {}
                norm_stats()
                for c in range(DC):
                    dve(lambda e, c=c: e.scalar_tensor_tensor(out=hTv(c, 0, NT), in0=xT[:, c, :], scalar=cst[:, 16 + c:17 + c],
                                                            in1=rstd[:], op0=ALU.mult, op1=ALU.mult),
                        (R_x[c], R_rstd, R_cst), (R_h[c],), (R_set[0], R_set[1]) if c == 0 else ())
                mT = lambda k, lo, hi: R3[:, (16 + k) * NT + lo:(16 + k) * NT + hi]
                winv = win.rearrange("(k p) n -> p k n", p=128)
                ZB = [rstd[:, 0:512], rstd[:, 512:1024], gcmp[:], tmpf[2][:]]
                RZ = [R_rstd, R_rstd, R_g, R_tmpf[2]]
                for grp in range(4):
                    for cp in range(2):
                        c0 = grp * 4 + cp * 2
                        for br, (gcol, pw, ofn, roff) in enumerate(((6144, pa_d, oA, 0), (8192, pb_d, oB, 8))):
                            sg_ = wload([(lambda t: ring_view_t(t, 16, 256), winv[:, :, gcol + c0 * 128:gcol + c0 * 128 + 256])])
                            sp_ = wload([(lambda t: ring_view_t(t, 8, 256), pw[:, c0 * 128:c0 * 128 + 256].rearrange("(k p) n -> p k n", p=128))])
                            wg = ring_view(sg_, 16, 256)
                            wp = ring_view(sp_, 8, 256)
                            for cc in range(2):
                                k = cp * 2 + cc
                                for th in range(2):
                                    zi = cc * 2 + th
                                    bg_, by_ = nbank(), nbank()
                                    for kk in range(DC):
                                        fl = (R_bank[bg_],) if kk in (0, DC - 1) else ()
                                        mm(banks[bg_][:], wg[:, kk, cc * 128:(cc + 1) * 128], hTv(kk, th * 512, (th + 1) * 512), kk == 0, kk == DC - 1,
                                           (R_ringp[sg_][0], R_ringp[sg_][1], R_h[kk]), fl)
                                    for hh in range(8):
                                        fl = (R_bank[by_],) if hh in (0, 7) else ()
                                        mm(banks[by_][:], wp[:, hh, cc * 128:(cc + 1) * 128], ofn(hh, th * 512, (th + 1) * 512), hh == 0, hh == 7,
                                           (R_ringp[sp_][0], R_ringp[sp_][1], R_r3[roff + hh]), fl)
                                    act(tmpf[br][:], banks[bg_][:], AF.Sigmoid, (R_bank[bg_],), (R_tmpf[br],))
                                    if br == 0:
                                        dve(lambda e, by_=by_, zi=zi: e.tensor_tensor(out=ZB[zi], in0=tmpf[0][:], in1=banks[by_][:], op=ALU.mult),
                                            (R_tmpf[0], R_bank[by_]), (RZ[zi],))
                                    else:
                                        dve(lambda e, by_=by_: e.tensor_tensor(out=tmpf[1][:], in0=tmpf[1][:], in1=banks[by_][:], op=ALU.mult),
                                            (R_tmpf[1], R_bank[by_]), (R_tmpf[1],))
                                        dve(lambda e, zi=zi, th=th, k=k: e.tensor_tensor(out=mT(k, th * 512, (th + 1) * 512), in0=tmpf[1][:], in1=ZB[zi], op=ALU.add),
                                            (R_tmpf[1], RZ[zi]), (R_r3[16 + k],))
                    for dt in range(8):
                        wv = wo_d[grp * 512:(grp + 1) * 512, dt * 256:(dt + 1) * 256].rearrange("(k p) n -> p k n", p=128)
                        s = wload([(lambda t: ring_view_t(t, 4, 256), wv)])
                        w = ring_view(s, 4, 256)
                        for cc in range(2):
                            c = dt * 2 + cc
                            for th in range(2):
                                b = nbank()
                                for k in range(4):
                                    fl = (R_bank[b],) if k in (0, 3) else ()
                                    mm(banks[b][:], w[:, k, cc * 128:(cc + 1) * 128], mT(k, th * 512, (th + 1) * 512), k == 0, k == 3,
                                       (R_ringp[s][0], R_ringp[s][1], R_r3[16 + k]), fl)
                                dve(lambda e, b=b, c=c, th=th: e.tensor_tensor(out=xT[:, c, th * 512:(th + 1) * 512], in0=banks[b][:],
                                                                             in1=xT[:, c, th * 512:(th + 1) * 512], op=ALU.add),
                                    (R_bank[b], R_x[c]), (R_x[c],))
                if debug:
                    for c in range(DC):
                        dma("sp", dbg["d_x2T"][c * 128:(c + 1) * 128, :], xT[:, c, :], (R_x[c],), (rd(f"dbgx2{c}"),))

        dump_small(2)
        if lvl >= 4:
            norm_to_h(32)
            ffn(wgu2, wdn2)
        norm_stats()
        for c in range(DC):
            dve(lambda e, c=c: e.scalar_tensor_tensor(out=xT[:, c, :], in0=xT[:, c, :], scalar=cst[:, 48 + c:49 + c],
                                                    in1=rstd[:], op0=ALU.mult, op1=ALU.mult),
                (R_x[c], R_rstd, R_cst), (R_x[c],))
            dma("sp", out_d[c * 128:(c + 1) * 128, :], xT[:, c, :], (R_x[c],), (rd(f"out{c}"),))
        outs = tuple(rd(f"out{c}") for c in range(DC)) + tuple(r for n, r in R_dram.items() if n.startswith("dbg"))
        P.add("sp", lambda e: e.nop(), outs, ())
        P.q["sp"][-1].signal = False

        P.finalize(None, dma_sems)
        P.prog_sems = {e: [es.enter_context(nc.semaphore(f"prog_{e}{i}")) for i in range(P.nbuckets[e])] for e in Prog.ENGS}
        with nc.Block() as block:
            @block.tensor
            def _(e):
                P.emit("pe", e)

            @block.scalar
            def _(e):
                P.emit("act", e)

            @block.vector
            def _(e):
                P.emit("dve", e)

            @block.gpsimd
            def _(e):
                P.emit("pool", e)

            @block.sync
            def _(e):
                P.emit("sp", e)
    return nc


def _tables(half):
    bf = ml_dtypes.bfloat16
    t = np.arange(2048)
    pos_k = np.where(t < 1024, t, half * 1024 + (t - 1024)).astype(np.float64)
    k_lo, k_hi = pos_k % 256, pos_k // 256
    vis = np.where((t < 1024) & (half == 0), NEG, 0.0)
    i = np.arange(NT)
    pos_q = (half * 1024 + i).astype(np.float64)
    q_lo, q_hi = pos_q % 256, pos_q // 256
    slopes = 2.0 ** (-(np.arange(8) + 1.0))
    qaug = np.stack([np.ones(NT), np.ones(NT), -q_lo, -q_hi, np.ones(NT)]).astype(bf)
    kaugA = np.zeros((8, 5, 2048), np.float64)
    kaugB = np.zeros((8, 69, 2048), np.float64)
    blk = t // 256
    ctx_dead = (t < 1024) & (half == 0)
    for h in range(8):
        rows = np.stack([slopes[h] * k_lo, slopes[h] * 256.0 * k_hi, np.full(2048, slopes[h]), np.full(2048, slopes[h] * 256.0), vis])
        kaugA[h] = rows
        kaugB[h, 64:69] = rows
        for s in range(8):
            kaugB[h, 8 * h + s] = ((blk == s) & ~ctx_dead).astype(np.float64)
    ptab = np.zeros((128, 3, 64), np.float32)
    for qt in range(8):
        for p in range(128):
            qi = qt * 128 + p
            jb = qi // 256
            for s in range(8):
                if s < 4:
                    valid = 1.0 if half == 1 else 0.0
                else:
                    valid = 1.0 if (s - 4) < jb else 0.0
                ptab[p, 0, qt * 8 + s] = 0.0 if valid else -1e9
                ptab[p, 1, qt * 8 + s] = valid
                ptab[p, 2, qt * 8 + s] = 1.0 if s == 4 + jb else 0.0
    kk, qq = np.meshgrid(np.arange(128), np.arange(128), indexing="ij")
    tri = (qq >= kk).astype(bf)
    return {"kaugA": kaugA.astype(bf), "qaugA": qaug, "kaugB": kaugB.astype(bf), "qaugB": qaug.copy(),
            "ptab": ptab, "tri": tri, "ident": np.eye(128, dtype=np.float32)}


def _make_in_maps(inp):
    f = lambda a: np.ascontiguousarray(np.asarray(a, dtype=np.float32))
    x = f(inp["x"])
    cst = np.zeros((128, 384), np.float32)
    for j, nm in enumerate(("g_ffn1", "g_mix", "g_ffn2")):
        cst[:, 16 * j:16 * j + 16] = f(inp[nm])[0].reshape(16, 128).T
    cst[:, 48:64] = f(inp["g_final"]).reshape(16, 128).T
    cst[:, 64] = f(inp["g_subln"])[0]
    for j, nm in enumerate(("lam_q1", "lam_k1", "lam_q2", "lam_k2")):
        cst[:, 128 + 64 * j:192 + 64 * j] = f(inp[nm])[0][None, :]
    shared = {
        "w_gu1": f(inp["w_ffn1_gu"])[0], "w_dn1": f(inp["w_ffn1_down"])[0], "w_in": f(inp["w_in"])[0],
        "p_a": f(inp["p_a"])[0], "p_b": f(inp["p_b"])[0], "w_o": f(inp["w_o"])[0],
        "w_gu2": f(inp["w_ffn2_gu"])[0], "w_dn2": f(inp["w_ffn2_down"])[0], "cst": cst,
    }
    tabs = [_tables(0), _tables(1)]
    maps = []
    for c in range(8):
        b, half = c // 2, c % 2
        m = dict(shared)
        m.update(tabs[half])
        m["xT"] = np.ascontiguousarray(x[b, half * 1024:(half + 1) * 1024, :].T)
        m["xcT"] = np.ascontiguousarray(x[b, 0:1024, :].T)
        maps.append(m)
    return maps


def kernel(**inputs):
    maps = _make_in_maps(inputs)
    nc = build(debug=False)
    res = run_bass_kernel_spmd(nc, maps, core_ids=list(range(8)))
    out = np.empty((4, 2048, D), np.float32)
    for c in range(8):
        b, half = c // 2, c % 2
        out[b, half * 1024:(half + 1) * 1024, :] = np.asarray(res.results[c]["outT"], dtype=np.float32).T
    return out
```

```python
import contextlib
import os
import numpy as np
import ml_dtypes
import concourse.bass as bass
import concourse.mybir as mybir
from concourse.bass_utils import run_bass_kernel_spmd

F32 = mybir.dt.float32
BF16 = mybir.dt.bfloat16
AF = mybir.ActivationFunctionType
ALU = mybir.AluOpType
AX = mybir.AxisListType

D = 2048
DC = 16
NT = 1024
FF = 5632
FCH = 44
EPS = 1e-6
NEG = -30000.0
LAM_INIT = 0.2
NBANK = 8
NSLOT = 4
SLOT_ELEMS = 4096


class Res:
    __slots__ = ("name", "w", "r")

    def __init__(self, name):
        self.name = name
        self.w = None
        self.r = {}


class Ins:
    __slots__ = ("eng", "fn", "deps", "signal", "val", "dma", "sem", "idx", "prewait")

    def __init__(self, eng, fn, dma):
        self.eng = eng
        self.fn = fn
        self.deps = {}
        self.signal = False
        self.val = None
        self.dma = dma
        self.sem = None
        self.idx = None
        self.prewait = None


class Prog:
    ENGS = ("pe", "act", "dve", "pool", "sp")
    BUCKET = 240

    def __init__(self):
        self.q = {e: [] for e in self.ENGS}
        self.n = 0

    def add(self, eng, fn, reads=(), writes=(), dma=False, after=()):
        ins = Ins(eng, fn, dma)
        ins.idx = self.n
        self.n += 1

        def dep(p):
            if p is None:
                return
            if p.eng == "pe" and eng == "pe" and not p.dma:
                return
            key = ("dma", p.idx) if p.dma else p.eng
            old = ins.deps.get(key)
            if old is None or old.idx < p.idx:
                ins.deps[key] = p

        for r in reads:
            dep(r.w)
        for w in tuple(writes) + tuple(after):
            dep(w.w)
            for rr in w.r.values():
                dep(rr)
        for r in reads:
            key = ("dma", ins.idx) if dma else eng
            r.r[key] = ins
        for w in writes:
            w.w = ins
            w.r = {}
        self.q[eng].append(ins)
        return ins

    def finalize(self, prog_sems, dma_sems):
        for e in self.ENGS:
            for ins in self.q[e]:
                for p in ins.deps.values():
                    p.signal = True
        cnt = {e: 0 for e in self.ENGS}
        dma_tot = [0] * len(dma_sems)
        dma_rr = 0
        for e in self.ENGS:
            pass
        allins = sorted((i for e in self.ENGS for i in self.q[e]), key=lambda i: i.idx)
        half = len(dma_sems) // 2
        rr = {"sp": 0, "pool": 0}
        for ins in allins:
            if ins.dma:
                base = 0 if ins.eng == "sp" else half
                s = base + rr[ins.eng] % half
                rr[ins.eng] += 1
                ins.sem = s
                ins.prewait = dma_tot[s]
                dma_tot[s] += 16
                ins.val = dma_tot[s]
            elif ins.signal:
                n = cnt[ins.eng]
                cnt[ins.eng] += 1
                ins.sem = n // self.BUCKET
                ins.val = n % self.BUCKET + 1
        self.nbuckets = {e: (cnt[e] + self.BUCKET - 1) // self.BUCKET for e in self.ENGS}
        self.dma_sems = dma_sems

    def emit(self, eng_name, eng):
        waited = {}

        def wait(sem, val):
            if val <= 0:
                return
            if waited.get(sem.name, 0) >= val:
                return
            waited[sem.name] = val
            eng.wait_ge(sem, val)

        for ins in self.q[eng_name]:
            for p in ins.deps.values():
                if p.dma:
                    wait(self.dma_sems[p.sem], p.val)
                else:
                    wait(self.prog_sems[p.eng][p.sem], p.val)
            if ins.dma:
                wait(self.dma_sems[ins.sem], ins.prewait)
                ins.fn(eng).then_inc(self.dma_sems[ins.sem], 16)
            else:
                r = ins.fn(eng)
                if ins.signal:
                    r.then_inc(self.prog_sems[ins.eng][ins.sem], 1)


def build(debug=False, lvl=9, fast=False, only=()):
    nc = bass.Bass("TRN2", target_bir_lowering=False)
    P = Prog()

    def din(name, shape, dt=F32):
        if fast and name.startswith(("w_", "p_")):
            return None
        return nc.dram_tensor(name, list(shape), dt, kind="ExternalInput").ap()

    xT_d = din("xT", [D, NT])
    xcT_d = din("xcT", [D, NT])
    wgu1 = din("w_gu1", [D, 2 * FF])
    wdn1 = din("w_dn1", [FF, D])
    win = din("w_in", [D, 10240])
    pa_d = din("p_a", [1024, D])
    pb_d = din("p_b", [1024, D])
    wo_d = din("w_o", [D, D])
    wgu2 = din("w_gu2", [D, 2 * FF])
    wdn2 = din("w_dn2", [FF, D])
    cst_d = din("cst", [128, 384])
    kaugA_d = din("kaugA", [8, 5, 2048], BF16)
    qaugA_d = din("qaugA", [5, NT], BF16)
    kaugB_d = din("kaugB", [8, 69, 2048], BF16)
    qaugB_d = din("qaugB", [5, NT], BF16)
    ptab_d = din("ptab", [128, 3, 64])
    tri_d = din("tri", [128, 128], BF16)
    ident_d = din("ident", [128, 128])
    out_d = nc.dram_tensor("outT", [D, NT], F32, kind="ExternalOutput").ap()

    KA_T = nc.dram_tensor("KA_T", [1024, 2048], BF16).ap()
    KB_T = nc.dram_tensor("KB_T", [1024, 2048], BF16).ap()
    VA_s = nc.dram_tensor("VA_s", [2048, 1024], BF16).ap()
    VB_s = nc.dram_tensor("VB_s", [2048, 1024], BF16).ap()
    QA_T = nc.dram_tensor("QA_T", [1024, NT], BF16).ap()
    QB_T = nc.dram_tensor("QB_T", [1024, NT], BF16).ap()
    dbg = {}
    if debug:
        for nm, shp in (("d_small0", [128, 16]), ("d_small1", [128, 16]), ("d_small2", [128, 16]), ("d_x1T", [D, NT]), ("d_oA0", [1024, NT]), ("d_oA", [1024, NT]), ("d_oB", [1024, NT]), ("d_x2T", [D, NT]),
                        ("d_sel", [64, NT])):
            dbg[nm] = nc.dram_tensor(nm, shp, F32, kind="ExternalOutput").ap()

    es = contextlib.ExitStack()
    with es:
        def sb(name, shape, dt):
            return es.enter_context(nc.sbuf_tensor(name, list(shape), dt))

        xT = sb("xT_sb", [128, DC, NT], F32)
        hT = sb("hT_sb", [128, DC * NT], BF16)
        R3 = sb("r3_sb", [128, 20 * 1024], BF16)
        ring = [sb(f"ring{i}", [128, SLOT_ELEMS], BF16) for i in range(NSLOT)]
        sq = [sb(f"sq{i}", [128, NT], BF16) for i in range(2)]
        rstd = sb("rstd", [128, NT], F32)
        tmpb = [sb(f"tmpb{i}", [128, 512], BF16) for i in range(6)]
        tmpf = [sb(f"tmpf{i}", [128, 512], F32) for i in range(3)]
        stage = [sb(f"stage{i}", [128, NT], BF16) for i in range(2)]
        cst = sb("cst_sb", [128, 384], F32)
        small = sb("small", [128, 16], F32)
        epsc = sb("epsc", [128, 1], F32)
        onesb = sb("onesb", [128, 128], BF16)
        onesD = sb("onesD", [128, 128], BF16)
        ones128 = sb("ones128", [128, 128], BF16)
        tri = sb("tri_sb", [128, 128], BF16)
        ident = sb("ident_sb", [128, 128], F32)
        ptab = sb("ptab_sb", [128, 3, 64], F32)
        ksum = sb("ksum", [128, 64], F32)
        ksh = sb("ksh", [128, 64], BF16)
        ksl = sb("ksl", [128, 64], BF16)
        gsc = sb("gsc", [128, 64], F32)
        gcmp = sb("gcmp", [128, 512], F32)
        grank = sb("grank", [128, 64], F32)
        qaugB = sb("qaugB_sb", [69, NT], BF16)
        banks = [es.enter_context(nc.psum_tensor(f"bank{i}", [128, 512], F32)) for i in range(NBANK)]

        dma_sems = [es.enter_context(nc.semaphore(f"dma{i}")) for i in range(24)]

        R_x = [Res(f"x{c}") for c in range(DC)]
        R_h = [Res(f"h{c}") for c in range(DC)]
        R_r3 = [Res(f"r3_{j}") for j in range(20)]
        R_ringp = [[Res(f"ring{i}a"), Res(f"ring{i}b")] for i in range(NSLOT)]
        R_sq = [Res("sq0"), Res("sq1")]
        R_rstd = Res("rstd")
        R_tmpb = [Res(f"tmpb{i}") for i in range(6)]
        R_tmpf = [Res(f"tmpf{i}") for i in range(3)]
        R_stage = [Res("stage0"), Res("stage1")]
        R_bank = [Res(f"bank{i}") for i in range(NBANK)]
        R_cst = Res("cst")
        R_small = Res("small")
        R_misc = Res("misc")
        R_g = Res("gtmp")
        R_qaugB = Res("qaugB")
        R_dram = {}

        def rd(name):
            if name not in R_dram:
                R_dram[name] = Res(name)
            return R_dram[name]

        st = {"bank": 0, "slot": 0, "tb": 0, "tf": 0, "stg": 0}

        def nbank():
            b = st["bank"]
            st["bank"] = (b + 1) % NBANK
            return b

        def ntb():
            b = st["tb"]
            st["tb"] = (b + 1) % 6
            return b

        def nstage():
            b = st["stg"]
            st["stg"] = (b + 1) % 2
            return b

        def mm(out, lhsT, rhs, start, stop, reads=(), writes=()):
            return P.add("pe", lambda e: e.matmul(out, lhsT, rhs, start=start, stop=stop), reads, writes)

        def act(out, in_, func, reads, writes, bias=None, scale=None):
            kw = {}
            if bias is not None:
                kw["bias"] = bias
            if scale is not None:
                kw["scale"] = scale
            return P.add("act", lambda e: e.activation(out, in_, func, **kw), reads, writes)

        def dve(fn, reads, writes, after=()):
            return P.add("dve", fn, reads, writes, after=after)

        def dma(q, out, in_, reads, writes, after=()):
            return P.add(q, lambda e: e.dma_start(out=out, in_=in_), reads, writes, dma=True, after=after)

        def wload(parts):
            s = st["slot"]
            st["slot"] = (s + 1) % NSLOT
            for i, (views, src) in enumerate(parts):
                dma("pool", views(ring[s]), src, (), (R_ringp[s][i],))
            return s

        def ring_view(s, K, ncols):
            return ring[s][:, 0:K * ncols].rearrange("p (k n) -> p k n", n=ncols)

        def hTv(c, lo, hi):
            return hT[:, c * NT + lo:c * NT + hi]

        dma("sp", cst[:], cst_d, (), (R_cst,))
        dma("sp", tri[:], tri_d, (), (R_misc,))
        dma("sp", ident[:], ident_d, (), (R_misc,))
        dma("sp", ptab[:], ptab_d, (), (R_misc,))
        dve(lambda e: e.memset(onesb[:], 1.0), (), (R_misc,))
        dve(lambda e: e.memset(epsc[:], EPS), (), (R_misc,))
        dve(lambda e: e.memset(onesD[:], 1.0 / D), (), (R_misc,))
        dve(lambda e: e.memset(ones128[:], 1.0 / 128.0), (), (R_misc,))
        dve(lambda e: e.tensor_tensor(out=gcmp[:, 0:64], in0=cst[:, 128:192], in1=cst[:, 192:256], op=ALU.mult), (R_cst,), (R_g,))
        dve(lambda e: e.tensor_reduce(out=small[:, 0:1], in_=gcmp[:, 0:64], axis=AX.X, op=ALU.add), (R_g,), (R_small,))
        dve(lambda e: e.tensor_tensor(out=gcmp[:, 64:128], in0=cst[:, 256:320], in1=cst[:, 320:384], op=ALU.mult), (R_cst,), (R_g,))
        dve(lambda e: e.tensor_reduce(out=small[:, 1:2], in_=gcmp[:, 64:128], axis=AX.X, op=ALU.add), (R_g,), (R_small,))
        act(small[:, 2:4], small[:, 0:2], AF.Exp, (R_small,), (R_small,))
        dve(lambda e: e.tensor_tensor(out=small[:, 4:5], in0=small[:, 3:4], in1=small[:, 2:3], op=ALU.subtract), (R_small,), (R_small,))
        dve(lambda e: e.tensor_scalar(out=small[:, 4:5], in0=small[:, 4:5], scalar1=-LAM_INIT, scalar2=None, op0=ALU.add), (R_small,), (R_small,))
        dve(lambda e: e.tensor_scalar(out=small[:, 5:6], in0=cst[:, 64:65], scalar1=1.0 - LAM_INIT, scalar2=None, op0=ALU.mult), (R_cst, R_small), (R_small,))

        dump_small = lambda i: dma("sp", dbg[f"d_small{i}"], small[:], (R_small,), (rd(f"dbgsmall{i}"),)) if debug else None
        dump_small(0)
        def load_x(src):
            for c4 in range(4):
                dma("sp", xT[:, c4 * 4:(c4 + 1) * 4, :], src.rearrange("(c p) t -> p c t", p=128)[:, c4 * 4:(c4 + 1) * 4, :],
                    (), tuple(R_x[c4 * 4:(c4 + 1) * 4]))

        def norm_stats():
            bs = [nbank(), nbank()]
            for c in range(DC):
                i = c % 2
                act(sq[i][:], xT[:, c, :], AF.Square, (R_x[c],), (R_sq[i],))
                for th in range(2):
                    mm(banks[bs[th]][:], onesD[:], sq[i][:, th * 512:(th + 1) * 512], c == 0, c == DC - 1,
                       (R_sq[i], R_misc), (R_bank[bs[th]],) if c in (0, DC - 1) else ())
            for th in range(2):
                b = bs[th]
                act(rstd[:, th * 512:(th + 1) * 512], banks[b][:], AF.Sqrt, (R_bank[b], R_misc), (R_rstd,), bias=epsc[:, 0:1])
                dve(lambda e, th=th: e.reciprocal(out=rstd[:, th * 512:(th + 1) * 512], in_=rstd[:, th * 512:(th + 1) * 512]), (R_rstd,), (R_rstd,))

        def norm_to_h(gcol0):
            norm_stats()
            for c in range(DC):
                dve(lambda e, c=c: e.scalar_tensor_tensor(out=hTv(c, 0, NT), in0=xT[:, c, :], scalar=cst[:, gcol0 + c:gcol0 + c + 1],
                                                        in1=rstd[:], op0=ALU.mult, op1=ALU.mult),
                    (R_x[c], R_rstd, R_cst), (R_h[c],))

        def ffn(wgu, wdn):
            hid = lambda j, lo, hi: R3[:, j * NT + lo:j * NT + hi]
            for qtr in range(4):
                for j in range(11):
                    f = qtr * 11 + j
                    wv = wgu.rearrange("(k p) n -> p k n", p=128)
                    s = wload([
                        (lambda t: ring_view_t(t, 16, 256)[:, :, 0:128], wv[:, :, f * 128:(f + 1) * 128]),
                        (lambda t: ring_view_t(t, 16, 256)[:, :, 128:256], wv[:, :, FF + f * 128:FF + (f + 1) * 128]),
                    ])
                    w = ring_view(s, 16, 256)
                    bg = [nbank(), nbank()]
                    bu = [nbank(), nbank()]
                    for k in range(DC):
                        for th in range(2):
                            fl = (R_bank[bg[th]],) if k in (0, DC - 1) else ()
                            mm(banks[bg[th]][:], w[:, k, 0:128], hTv(k, th * 512, (th + 1) * 512), k == 0, k == DC - 1,
                               (R_ringp[s][0], R_ringp[s][1], R_h[k]), fl)
                        for th in range(2):
                            fl = (R_bank[bu[th]],) if k in (0, DC - 1) else ()
                            mm(banks[bu[th]][:], w[:, k, 128:256], hTv(k, th * 512, (th + 1) * 512), k == 0, k == DC - 1,
                               (R_ringp[s][0], R_ringp[s][1], R_h[k]), fl)
                    for th in range(2):
                        tb = ntb()
                        act(tmpb[tb][:], banks[bg[th]][:], AF.Silu, (R_bank[bg[th]],), (R_tmpb[tb],))
                        dve(lambda e, tb=tb, th=th, j=j, b=bu[th]: e.tensor_tensor(out=hid(j, th * 512, (th + 1) * 512), in0=tmpb[tb][:],
                                                                                 in1=banks[b][:], op=ALU.mult),
                            (R_tmpb[tb], R_bank[bu[th]]), (R_r3[j],))
                for dt in range(8):
                    wv = wdn[qtr * 11 * 128:(qtr + 1) * 11 * 128, dt * 256:(dt + 1) * 256].rearrange("(j p) n -> p j n", p=128)
                    s = wload([(lambda t: ring_view_t(t, 11, 256), wv)])
                    w = ring_view(s, 11, 256)
                    for cc in range(2):
                        c = dt * 2 + cc
                        for th in range(2):
                            b = nbank()
                            for j in range(11):
                                fl = (R_bank[b],) if j in (0, 10) else ()
                                mm(banks[b][:], w[:, j, cc * 128:(cc + 1) * 128], hid(j, th * 512, (th + 1) * 512), j == 0, j == 10,
                                   (R_ringp[s][0], R_ringp[s][1], R_r3[j]), fl)
                            dve(lambda e, b=b, c=c, th=th: e.scalar_tensor_tensor(out=xT[:, c, th * 512:(th + 1) * 512], in0=banks[b][:], scalar=0.5,
                                                                                in1=xT[:, c, th * 512:(th + 1) * 512], op0=ALU.mult, op1=ALU.add),
                                (R_bank[b], R_x[c]), (R_x[c],))

        def ring_view_t(t, K, ncols):
            return t[:, 0:K * ncols].rearrange("p (k n) -> p k n", n=ncols)

        def proj_T(col0, dst, tok0, scale, key):
            wv = win.rearrange("(k p) n -> p k n", p=128)
            for t in range(4):
                s = wload([(lambda tt: ring_view_t(tt, 16, 256), wv[:, :, col0 + t * 256:col0 + (t + 1) * 256])])
                w = ring_view(s, 16, 256)
                for cc in range(2):
                    ch = t * 2 + cc
                    bs = [nbank(), nbank()]
                    for k in range(DC):
                        for th in range(2):
                            fl = (R_bank[bs[th]],) if k in (0, DC - 1) else ()
                            mm(banks[bs[th]][:], w[:, k, cc * 128:(cc + 1) * 128], hTv(k, th * 512, (th + 1) * 512), k == 0, k == DC - 1,
                               (R_ringp[s][0], R_ringp[s][1], R_h[k]), fl)
                    sg = nstage()
                    for th in range(2):
                        if th == 0:
                            act(stage[sg][:, 0:512], banks[bs[0]][:], AF.Identity, (R_bank[bs[0]],), (R_stage[sg],), scale=scale)
                        else:
                            dve(lambda e, sg=sg, b=bs[1]: e.tensor_scalar(out=stage[sg][:, 512:1024], in0=banks[b][:], scalar1=scale, scalar2=None, op0=ALU.mult),
                                (R_bank[bs[1]],), (R_stage[sg],))
                    dma("sp", dst[ch * 128:(ch + 1) * 128, tok0:tok0 + NT], stage[sg][:], (R_stage[sg],), (rd(f"{key}{ch}"),))

        def proj_V(col0, dst, tok0, key):
            wv = win.rearrange("(k p) n -> p k n", p=128)
            for t in range(4):
                s = wload([(lambda tt: ring_view_t(tt, 16, 256), wv[:, :, col0 + t * 256:col0 + (t + 1) * 256])])
                w = ring_view(s, 16, 256)
                for g4 in range(2):
                    sg = nstage()
                    for ti in range(4):
                        tt = g4 * 4 + ti
                        b = nbank()
                        for k in range(DC):
                            fl = (R_bank[b],) if k in (0, DC - 1) else ()
                            mm(banks[b][:, 0:256], hTv(k, tt * 128, (tt + 1) * 128), w[:, k, :], k == 0, k == DC - 1,
                               (R_ringp[s][0], R_ringp[s][1], R_h[k]), fl)
                        if ti % 2 == 0:
                            act(stage[sg][:, ti * 256:(ti + 1) * 256], banks[b][:, 0:256], AF.Copy, (R_bank[b],), (R_stage[sg],))
                        else:
                            dve(lambda e, sg=sg, b=b, ti=ti: e.tensor_copy(out=stage[sg][:, ti * 256:(ti + 1) * 256], in_=banks[b][:, 0:256]),
                                (R_bank[b],), (R_stage[sg],))
                    dview = dst[tok0 + g4 * 512:tok0 + (g4 + 1) * 512, t * 256:(t + 1) * 256].rearrange("(i p) n -> p i n", p=128)
                    dma("sp", dview, stage[sg][:].rearrange("p (i n) -> p i n", n=256), (R_stage[sg],), (rd(f"{key}{t}_{g4}_{tok0}"),))

        def bcast_mid(ap2, n):
            return ap2.unsqueeze(1).broadcast_to([128, n, ap2.shape[1]])

        def bcast_g_other(g):
            return g[:].rearrange("p (h s) -> p h s", s=8).unsqueeze(2).broadcast_to([128, 8, 8, 8])

        def bcast_g_self(g):
            return g[:].rearrange("p (h s) -> p h s", s=8).unsqueeze(3).broadcast_to([128, 8, 8, 8])

        def dump(name, sb_ap, reads):
            if debug:
                dma("sp", dbg[name], sb_ap, reads, (rd("dbg_" + name),))

        if fast:
            ffn = lambda *a, **k: None
            proj_T = lambda *a, **k: None
            proj_V = lambda *a, **k: None
        if lvl >= 2:
            load_x(xcT_d)
            norm_to_h(0)
            ffn(wgu1, wdn1)
            norm_to_h(16)
            proj_T(1024, KA_T, 0, 1.0, "ka0_")
            proj_V(2048, VA_s, 0, "va")
            proj_T(4096, KB_T, 0, 1.0, "kb0_")
            proj_V(5120, VB_s, 0, "vb")
        load_x(xT_d)
        norm_to_h(0)
        ffn(wgu1, wdn1)
        if debug:
            for c in range(DC):
                dma("sp", dbg["d_x1T"][c * 128:(c + 1) * 128, :], xT[:, c, :], (R_x[c],), (rd(f"dbgx1{c}"),))
        if lvl >= 3:
            norm_to_h(16)
            proj_T(0, QA_T, 0, 0.125, "qa_")
            proj_T(1024, KA_T, NT, 1.0, "ka1_")
            proj_V(2048, VA_s, NT, "va")
            proj_T(3072, QB_T, 0, 128.0 ** -0.5, "qb_")
            proj_T(4096, KB_T, NT, 1.0, "kb1_")
            proj_V(5120, VB_s, NT, "vb")

            all_h = tuple(R_h)
            KVQ = hT

            def kvq(off, n):
                return KVQ[:, off:off + n]

            all_dram_k = lambda pre: tuple(rd(f"{pre}0_{ch}") for ch in range(8)) + tuple(rd(f"{pre}1_{ch}") for ch in range(8))
            all_dram_v = lambda pre: tuple(rd(f"{pre}{t}_{g}_{tk}") for t in range(4) for g in range(2) for tk in (0, NT))
            oA = lambda h, lo, hi: R3[:, h * NT + lo:h * NT + hi]
            oB = lambda h, lo, hi: R3[:, (8 + h) * NT + lo:(8 + h) * NT + hi]

            SB = [0, 1, 2]
            OB = [3, 4]
            SMB = [5, 6]
            XB = 7
            sidx = [0]

            def ktiles(qc):
                lst = [(kt, 0, False) for kt in range(8)]
                for j in range(4 * qc + 4):
                    if j < 4 * qc:
                        lst.append((8 + j, 0, False))
                    else:
                        lst.append((8 + j, 128 * (j - 4 * qc), True))
                return lst

            NTB = len(tmpb)
            SBK = [0, 1, 2, 7]

            def run_tiles(tiles, LOOK=4):
                tbs = {}
                n = len(tiles)

                def issue(i):
                    t = tiles[i]
                    b = SBK[sidx[0] % len(SBK)]
                    sidx[0] += 1
                    col0 = t["col0"]
                    t["score"](b)
                    tb = st["tb"]
                    st["tb"] = (tb + 1) % NTB
                    tbs[i] = tb
                    act(tmpb[tb][:, col0:512], banks[b][:, col0:512], AF.Exp, (R_bank[b],), (R_tmpb[tb],))
                    if t["diag"]:
                        dve(lambda e, tb=tb, col0=col0: e.tensor_tensor(out=tmpb[tb][:, col0:col0 + 128], in0=tmpb[tb][:, col0:col0 + 128], in1=tri[:], op=ALU.mult),
                            (R_tmpb[tb], R_misc), (R_tmpb[tb],))

                deferred = []
                st["defer"] = lambda k, fn: deferred.append([k, fn])
                for i in range(min(LOOK, n)):
                    issue(i)
                for i, t in enumerate(tiles):
                    if i + LOOK < n:
                        issue(i + LOOK)
                    for d in [d for d in deferred if d[0] <= 0]:
                        deferred.remove(d)
                        d[1]()
                    for d in deferred:
                        d[0] -= 1
                    tb = tbs[i]
                    col0, ob, smb, first, last = t["col0"], t["ob"], t["smb"], t["first"], t["last"]
                    mm(banks[ob][:, col0:512], t["V"], tmpb[tb][:, col0:512], first, last,
                       (R_tmpb[tb],) + tuple(t["_ld"][t["head"]]), (R_bank[ob],) if (first or last) else ())
                    mm(banks[smb][:, col0:512], onesb[:], tmpb[tb][:, col0:512], first, last,
                       (R_tmpb[tb], R_misc), (R_bank[smb],) if (first or last) else ())
                    for p in t["post"]:
                        p()
                for d in deferred:
                    d[1]()

            R_set = [Res("kvqset0"), Res("kvqset1")]
            if lvl >= 5:
                def bufsA(h):
                    base = (h % 2) * 8192
                    KA = [KVQ[0:69, base + m * 2048:base + (m + 1) * 2048] for m in range(2)]
                    QA = [KVQ[0:69, base + 4096 + m * 1024:base + 4096 + (m + 1) * 1024] for m in range(2)]
                    VAh = KVQ[:, base + 6144:base + 8192]
                    return KA, QA, VAh

                ldA = {}

                def loadA(h):
                    KA, QA, VAh = bufsA(h)
                    rs = R_set[h % 2]
                    aft = (rs,) + (all_h if h < 2 else ())
                    ld = [Res(f"ldA{h}_{i}") for i in range(9)]
                    for m in range(2):
                        r0 = h * 128 + m * 64
                        dma("sp", KA[m][0:64, :], KA_T[r0:r0 + 64, :], all_dram_k("ka"), (ld[4 * m],), aft)
                        dma("sp", KA[m][64:69, :], kaugA_d[h], (), (ld[4 * m + 1],), aft)
                        dma("sp", QA[m][0:64, :], QA_T[r0:r0 + 64, :], tuple(rd(f"qa_{ch}") for ch in range(8)), (ld[4 * m + 2],), aft)
                        dma("sp", QA[m][64:69, :], qaugA_d, (), (ld[4 * m + 3],), aft)
                    dma("sp", VAh.rearrange("p (t d) -> p t d", d=128), VA_s[:, h * 128:(h + 1) * 128].rearrange("(t p) d -> p t d", p=128),
                        all_dram_v("va"), (ld[8],), aft)
                    ldA[h] = (rs,) + tuple(ld)

                def postA0():
                    t0 = tmpf[0]
                    dve(lambda e, t0=t0: e.reciprocal(out=t0[:], in_=banks[SMB[0]][:]), (R_bank[SMB[0]],), (R_tmpf[0],))
                    dve(lambda e, t0=t0: e.tensor_tensor(out=t0[:], in0=banks[OB[0]][:], in1=t0[:], op=ALU.mult), (R_bank[OB[0]], R_tmpf[0]), (R_tmpf[0],))

                def postA1(h, qc):
                    t0, t1 = tmpf[0], tmpf[1]
                    dve(lambda e, t1=t1: e.reciprocal(out=t1[:], in_=banks[SMB[1]][:]), (R_bank[SMB[1]],), (R_tmpf[1],))
                    dve(lambda e, t1=t1: e.tensor_tensor(out=t1[:], in0=banks[OB[1]][:], in1=t1[:], op=ALU.mult), (R_bank[OB[1]], R_tmpf[1]), (R_tmpf[1],))
                    dve(lambda e, t0=t0, t1=t1: e.scalar_tensor_tensor(out=t0[:], in0=t1[:], scalar=small[:, 4:5], in1=t0[:], op0=ALU.mult, op1=ALU.add),
                        (R_tmpf[0], R_tmpf[1], R_small), (R_tmpf[0],))
                    tb = st["tb"]
                    st["tb"] = (tb + 1) % NTB
                    dve(lambda e, t0=t0, tb=tb: e.tensor_tensor(out=tmpb[tb][:], in0=t0[:], in1=t0[:], op=ALU.mult), (R_tmpf[0],), (R_tmpb[tb],))
                    sqb = tmpf[2]

                    def stage2(h=h, qc=qc, t0=t0, t1=t1, tb=tb):
                        xb = SBK[sidx[0] % len(SBK)]
                        sidx[0] += 1
                        mm(banks[xb][:], ones128[:], tmpb[tb][:], True, True, (R_tmpb[tb], R_misc), (R_bank[xb],))
                        act(t1[:], banks[xb][:], AF.Ln, (R_bank[xb], R_misc), (R_tmpf[1],), bias=epsc[:, 0:1])
                        act(t1[:], t1[:], AF.Exp, (R_tmpf[1],), (R_tmpf[1],), scale=-0.5)
                        dve(lambda e, h=h, qc=qc, t0=t0, t1=t1: e.scalar_tensor_tensor(out=oA(h, qc * 512, (qc + 1) * 512), in0=t0[:], scalar=small[:, 5:6], in1=t1[:],
                                                                                   op0=ALU.mult, op1=ALU.mult),
                            (R_tmpf[0], R_tmpf[1], R_small), (R_r3[h],))
                    st["defer"](4, stage2)

                loadA(0)
                loadA(1)
                tilesA = []
                for h in range(8):
                    KA, QA, VAh = bufsA(h)
                    for qc in range(2):
                        for m in range(2):
                            kl = ktiles(qc)
                            for i, (kt, col0, diag) in enumerate(kl):
                                def score(b, h=h, m=m, qc=qc, kt=kt, col0=col0, KA=KA, QA=QA):
                                    mm(banks[b][:, col0:512], KA[m][:, kt * 128:(kt + 1) * 128], QA[m][:, qc * 512 + col0:(qc + 1) * 512], True, True,
                                       ldA[h], (R_bank[b],))
                                post = []
                                if i == len(kl) - 1:
                                    if m == 0:
                                        post.append(postA0)
                                    else:
                                        post.append(lambda h=h, qc=qc: postA1(h, qc))
                                        if qc == 1 and h + 2 < 8:
                                            post.append(lambda h=h: loadA(h + 2))
                                tilesA.append(dict(score=score, col0=col0, diag=diag, V=VAh[:, kt * 128:(kt + 1) * 128], ob=OB[m], smb=SMB[m],
                                                   first=i == 0, last=i == len(kl) - 1, reads_v=None, post=post, head=h))
                for t in tilesA:
                    t["_ld"] = ldA
                run_tiles(tilesA)


            dump_small(1)
            if debug and lvl >= 5:
                for h in range(8):
                    for qc in range(2):
                        dve(lambda e, h=h, qc=qc: e.tensor_copy(out=tmpf[2][:], in_=oA(h, qc * 512, (qc + 1) * 512)), (R_r3[h],), (R_tmpf[2],))
                        dma("sp", dbg["d_oA0"][h * 128:(h + 1) * 128, qc * 512:(qc + 1) * 512], tmpf[2][:], (R_tmpf[2],), (rd(f"dbgoA0{h}{qc}"),))
            if lvl >= 6:
                GS = int(os.environ.get("GSTEP", "99"))
                R_qball = Res("qball")
                rsq = R_qball
                rsk = [Res("kbtmp0"), Res("kbtmp1")]
                QBall = KVQ[:, 0:8192]
                dma("sp", QBall.rearrange("p (h t) -> p h t", t=NT), QB_T.rearrange("(h p) t -> p h t", p=128),
                    tuple(rd(f"qb_{ch}") for ch in range(8)), (R_qball,), (R_set[0], R_set[1]))
                dma("sp", qaugB[64:69, :], qaugB_d, (), (R_qaugB,))
                for h in range(8 if GS >= 2 else 0):
                    kb = KVQ[:, 8192 + (h % 2) * 2048:8192 + (h % 2 + 1) * 2048]
                    dma("sp", kb, KB_T[h * 128:(h + 1) * 128, :], all_dram_k("kb"), (rsk[h % 2],), (R_set[1],))
                    dve(lambda e, kb=kb, h=h: e.tensor_reduce(out=ksum[:, h * 8:(h + 1) * 8], in_=kb.rearrange("p (s t) -> p s t", t=256), axis=AX.X, op=ALU.add),
                        (rsk[h % 2], R_set[1]), (R_g,))
                dve(lambda e: e.tensor_copy(out=ksh[:], in_=ksum[:]), (R_g,), (R_g,))
                dve(lambda e: e.tensor_tensor(out=gsc[:], in0=ksum[:], in1=ksh[:], op=ALU.subtract), (R_g,), (R_g,))
                dve(lambda e: e.tensor_copy(out=ksl[:], in_=gsc[:]), (R_g,), (R_g,))
                gb = nbank()
                for qt in range(8):
                    for h in range(8):
                        fl = (R_bank[gb],) if (qt, h) in ((0, 0), (7, 7)) else ()
                        o_ = banks[gb][:, qt * 64 + h * 8:qt * 64 + (h + 1) * 8]
                        q_ = QBall[:, h * NT + qt * 128:h * NT + (qt + 1) * 128]
                        mm(o_, q_, ksh[:, h * 8:(h + 1) * 8], True, False, (rsq, R_g, R_set[0], R_set[1]), fl)
                        mm(o_, q_, ksl[:, h * 8:(h + 1) * 8], False, True, (rsq, R_g, R_set[0], R_set[1]), fl)
                scr = tuple(R_r3[8:18])
                gcmp_all = R3[:, 8 * 1024:16 * 1024].bitcast(F32)
                gall = R3[:, 16 * 1024:17 * 1024].bitcast(F32)
                grk = R3[:, 17 * 1024:18 * 1024].bitcast(F32)
                pt4 = lambda w: ptab[:, w, :].rearrange("p (q s) -> p q s", s=8).unsqueeze(2).broadcast_to([128, 8, 8, 8])
                v4 = lambda ap: ap.rearrange("p (q h s) -> p q h s", h=8, s=8)
                dve(lambda e: e.tensor_tensor(out=v4(gall), in0=v4(banks[gb][:]), in1=pt4(0), op=ALU.add), (R_bank[gb], R_misc), scr)
                a3 = lambda ap: ap.rearrange("p (a s) -> p a s", s=8)
                dve(lambda e: e.tensor_tensor(out=gcmp_all.rearrange("p (a s t) -> p a s t", s=8, t=8),
                                              in0=a3(gall).unsqueeze(2).broadcast_to([128, 64, 8, 8]),
                                              in1=a3(gall).unsqueeze(3).broadcast_to([128, 64, 8, 8]), op=ALU.is_gt), scr, scr)
                dve(lambda e: e.tensor_reduce(out=grk, in_=gcmp_all.rearrange("p (a t) -> p a t", t=8), axis=AX.X, op=ALU.add), scr, scr)
                dve(lambda e: e.tensor_single_scalar(out=grk, in_=grk, scalar=2.5, op=ALU.is_lt), scr, scr)
                dve(lambda e: e.tensor_tensor(out=v4(grk), in0=v4(grk), in1=pt4(1), op=ALU.mult), scr + (R_misc,), scr)
                dve(lambda e: e.tensor_tensor(out=v4(grk), in0=v4(grk), in1=pt4(2), op=ALU.add), scr + (R_misc,), scr)
                dve(lambda e: e.tensor_scalar(out=grk, in0=grk, scalar1=-1.0, scalar2=-NEG, op0=ALU.add, op1=ALU.mult), scr, scr)
                for half in range(2):
                    b2 = nbank()
                    for j in range(4):
                        qt = half * 4 + j
                        P.add("pe", lambda e, b2=b2, j=j, qt=qt: e.transpose(banks[b2][0:64, j * 128:(j + 1) * 128], grk[:, qt * 64:(qt + 1) * 64], ident[:]),
                              scr + (R_misc,), (R_bank[b2],) if j in (0, 3) else ())
                    act(qaugB[0:64, half * 512:(half + 1) * 512], banks[b2][0:64, :], AF.Copy, (R_bank[b2],), (R_qaugB,))

            if lvl >= 7:
                def bufsB(h):
                    base = (h % 2) * 8192
                    return (KVQ[:, base:base + 2048], KVQ[0:69, base + 2048:base + 4096], KVQ[:, base + 4096:base + 5120], KVQ[:, base + 5120:base + 7168])

                ldB = {}

                def loadB(h):
                    KBh, KAUG, QBh, VBh = bufsB(h)
                    rs = R_set[h % 2]
                    ld = [Res(f"ldB{h}_{i}") for i in range(4)]
                    dma("sp", KBh, KB_T[h * 128:(h + 1) * 128, :], all_dram_k("kb"), (ld[0],), (rs,))
                    dma("sp", KAUG, kaugB_d[h], (), (ld[1],), (rs,))
                    dma("sp", QBh, QB_T[h * 128:(h + 1) * 128, :], tuple(rd(f"qb_{ch}") for ch in range(8)), (ld[2],), (rs,))
                    dma("sp", VBh.rearrange("p (t d) -> p t d", d=128), VB_s[:, h * 128:(h + 1) * 128].rearrange("(t p) d -> p t d", p=128),
                        all_dram_v("vb"), (ld[3],), (rs,))
                    ldB[h] = (rs,) + tuple(ld)

                def postB(h, qc):
                    t0 = tmpf[qc]
                    dve(lambda e, t0=t0, qc=qc: e.reciprocal(out=t0[:], in_=banks[SMB[qc]][:]), (R_bank[SMB[qc]],), (R_tmpf[qc],))
                    dve(lambda e, t0=t0, qc=qc, h=h: e.tensor_tensor(out=oB(h, qc * 512, (qc + 1) * 512), in0=banks[OB[qc]][:], in1=t0[:], op=ALU.mult),
                        (R_bank[OB[qc]], R_tmpf[qc]), (R_r3[8 + h],))

                loadB(0)
                loadB(1)
                tilesB = []
                for h in range(8):
                    KBh, KAUG, QBh, VBh = bufsB(h)
                    for qc in range(2):
                        kl = ktiles(qc)
                        for i, (kt, col0, diag) in enumerate(kl):
                            def score(b, h=h, qc=qc, kt=kt, col0=col0, KBh=KBh, KAUG=KAUG, QBh=QBh):
                                mm(banks[b][:, col0:512], KBh[:, kt * 128:(kt + 1) * 128], QBh[:, qc * 512 + col0:(qc + 1) * 512], True, False,
                                   ldB[h], (R_bank[b],))
                                mm(banks[b][:, col0:512], KAUG[:, kt * 128:(kt + 1) * 128], qaugB[:, qc * 512 + col0:(qc + 1) * 512], False, True,
                                   ldB[h] + (R_qaugB,), (R_bank[b],))
                            post = []
                            if i == len(kl) - 1:
                                post.append(lambda h=h, qc=qc: postB(h, qc))
                                if qc == 1 and h + 2 < 8:
                                    post.append(lambda h=h: loadB(h + 2))
                            tilesB.append(dict(score=score, col0=col0, diag=diag, V=VBh[:, kt * 128:(kt + 1) * 128], ob=OB[qc], smb=SMB[qc],
                                               first=i == 0, last=i == len(kl) - 1, reads_v=(), post=post, head=h, _ld=ldB))
                run_tiles(tilesB)

            if debug:
                for h in range(8):
                    for nm, f, off in (("d_oA", oA, 0), ("d_oB", oB, 8)):
                        for qc in range(2):
                            dve(lambda e, f=f, h=h, qc=qc: e.tensor_copy(out=tmpf[2][:], in_=f(h, qc * 512, (qc + 1) * 512)), (R_r3[off + h],), (R_tmpf[2],))
                            dma("sp", dbg[nm][h * 128:(h + 1) * 128, qc * 512:(qc + 1) * 512], tmpf[2][:], (R_tmpf[2],), (rd(f"dbg{nm}{h}{qc}"),))

            if lvl >= 8:
                for c in range(DC):
                    R_h[c].w = None
                    R_h[c].r = {}
                norm_stats()
                for c in range(DC):
                    dve(lambda e, c=c: e.scalar_tensor_tensor(out=hTv(c, 0, NT), in0=xT[:, c, :], scalar=cst[:, 16 + c:17 + c],
                                                            in1=rstd[:], op0=ALU.mult, op1=ALU.mult),
                        (R_x[c], R_rstd, R_cst), (R_h[c],), (R_set[0], R_set[1]) if c == 0 else ())
                mT = lambda k, lo, hi: R3[:, (16 + k) * NT + lo:(16 + k) * NT + hi]
                winv = win.rearrange("(k p) n -> p k n", p=128)
                ZB = [rstd[:, 0:512], rstd[:, 512:1024], gcmp[:], tmpf[2][:]]
                RZ = [R_rstd, R_rstd, R_g, R_tmpf[2]]
                for grp in range(4):
                    for cp in range(2):
                        c0 = grp * 4 + cp * 2
                        for br, (gcol, pw, ofn, roff) in enumerate(((6144, pa_d, oA, 0), (8192, pb_d, oB, 8))):
                            sg_ = wload([(lambda t: ring_view_t(t, 16, 256), winv[:, :, gcol + c0 * 128:gcol + c0 * 128 + 256])])
                            sp_ = wload([(lambda t: ring_view_t(t, 8, 256), pw[:, c0 * 128:c0 * 128 + 256].rearrange("(k p) n -> p k n", p=128))])
                            wg = ring_view(sg_, 16, 256)
                            wp = ring_view(sp_, 8, 256)
                            for cc in range(2):
                                k = cp * 2 + cc
                                for th in range(2):
                                    zi = cc * 2 + th
                                    bg_, by_ = nbank(), nbank()
                                    for kk in range(DC):
                                        fl = (R_bank[bg_],) if kk in (0, DC - 1) else ()
                                        mm(banks[bg_][:], wg[:, kk, cc * 128:(cc + 1) * 128], hTv(kk, th * 512, (th + 1) * 512), kk == 0, kk == DC - 1,
                                           (R_ringp[sg_][0], R_ringp[sg_][1], R_h[kk]), fl)
                                    for hh in range(8):
                                        fl = (R_bank[by_],) if hh in (0, 7) else ()
                                        mm(banks[by_][:], wp[:, hh, cc * 128:(cc + 1) * 128], ofn(hh, th * 512, (th + 1) * 512), hh == 0, hh == 7,
                                           (R_ringp[sp_][0], R_ringp[sp_][1], R_r3[roff + hh]), fl)
                                    act(tmpf[br][:], banks[bg_][:], AF.Sigmoid, (R_bank[bg_],), (R_tmpf[br],))
                                    if br == 0:
                                        dve(lambda e, by_=by_, zi=zi: e.tensor_tensor(out=ZB[zi], in0=tmpf[0][:], in1=banks[by_][:], op=ALU.mult),
                                            (R_tmpf[0], R_bank[by_]), (RZ[zi],))
                                    else:
                                        dve(lambda e, by_=by_: e.tensor_tensor(out=tmpf[1][:], in0=tmpf[1][:], in1=banks[by_][:], op=ALU.mult),
                                            (R_tmpf[1], R_bank[by_]), (R_tmpf[1],))
                                        dve(lambda e, zi=zi, th=th, k=k: e.tensor_tensor(out=mT(k, th * 512, (th + 1) * 512), in0=tmpf[1][:], in1=ZB[zi], op=ALU.add),
                                            (R_tmpf[1], RZ[zi]), (R_r3[16 + k],))
                    for dt in range(8):
                        wv = wo_d[grp * 512:(grp + 1) * 512, dt * 256:(dt + 1) * 256].rearrange("(k p) n -> p k n", p=128)
                        s = wload([(lambda t: ring_view_t(t, 4, 256), wv)])
                        w = ring_view(s, 4, 256)
                        for cc in range(2):
                            c = dt * 2 + cc
                            for th in range(2):
                                b = nbank()
                                for k in range(4):
                                    fl = (R_bank[b],) if k in (0, 3) else ()
                                    mm(banks[b][:], w[:, k, cc * 128:(cc + 1) * 128], mT(k, th * 512, (th + 1) * 512), k == 0, k == 3,
                                       (R_ringp[s][0], R_ringp[s][1], R_r3[16 + k]), fl)
                                dve(lambda e, b=b, c=c, th=th: e.tensor_tensor(out=xT[:, c, th * 512:(th + 1) * 512], in0=banks[b][:],
                                                                             in1=xT[:, c, th * 512:(th + 1) * 512], op=ALU.add),
                                    (R_bank[b], R_x[c]), (R_x[c],))
                if debug:
                    for c in range(DC):
                        dma("sp", dbg["d_x2T"][c * 128:(c + 1) * 128, :], xT[:, c, :], (R_x[c],), (rd(f"dbgx2{c}"),))

        dump_small(2)
        if lvl >= 4:
            norm_to_h(32)
            ffn(wgu2, wdn2)
        norm_stats()
        for c in range(DC):
            dve(lambda e, c=c: e.scalar_tensor_tensor(out=xT[:, c, :], in0=xT[:, c, :], scalar=cst[:, 48 + c:49 + c],
                                                    in1=rstd[:], op0=ALU.mult, op1=ALU.mult),
                (R_x[c], R_rstd, R_cst), (R_x[c],))
            dma("sp", out_d[c * 128:(c + 1) * 128, :], xT[:, c, :], (R_x[c],), (rd(f"out{c}"),))
        outs = tuple(rd(f"out{c}") for c in range(DC)) + tuple(r for n, r in R_dram.items() if n.startswith("dbg"))
        P.add("sp", lambda e: e.nop(), outs, ())
        P.q["sp"][-1].signal = False

        P.finalize(None, dma_sems)
        P.prog_sems = {e: [es.enter_context(nc.semaphore(f"prog_{e}{i}")) for i in range(P.nbuckets[e])] for e in Prog.ENGS}
        with nc.Block() as block:
            @block.tensor
            def _(e):
                P.emit("pe", e)

            @block.scalar
            def _(e):
                P.emit("act", e)

            @block.vector
            def _(e):
                P.emit("dve", e)

            @block.gpsimd
            def _(e):
                P.emit("pool", e)

            @block.sync
            def _(e):
                P.emit("sp", e)
    return nc


def _tables(half):
    bf = ml_dtypes.bfloat16
    t = np.arange(2048)
    pos_k = np.where(t < 1024, t, half * 1024 + (t - 1024)).astype(np.float64)
    k_lo, k_hi = pos_k % 256, pos_k // 256
    vis = np.where((t < 1024) & (half == 0), NEG, 0.0)
    i = np.arange(NT)
    pos_q = (half * 1024 + i).astype(np.float64)
    q_lo, q_hi = pos_q % 256, pos_q // 256
    slopes = 2.0 ** (-(np.arange(8) + 1.0))
    qaug = np.stack([np.ones(NT), np.ones(NT), -q_lo, -q_hi, np.ones(NT)]).astype(bf)
    kaugA = np.zeros((8, 5, 2048), np.float64)
    kaugB = np.zeros((8, 69, 2048), np.float64)
    blk = t // 256
    ctx_dead = (t < 1024) & (half == 0)
    for h in range(8):
        rows = np.stack([slopes[h] * k_lo, slopes[h] * 256.0 * k_hi, np.full(2048, slopes[h]), np.full(2048, slopes[h] * 256.0), vis])
        kaugA[h] = rows
        kaugB[h, 64:69] = rows
        for s in range(8):
            kaugB[h, 8 * h + s] = ((blk == s) & ~ctx_dead).astype(np.float64)
    ptab = np.zeros((128, 3, 64), np.float32)
    for qt in range(8):
        for p in range(128):
            qi = qt * 128 + p
            jb = qi // 256
            for s in range(8):
                if s < 4:
                    valid = 1.0 if half == 1 else 0.0
                else:
                    valid = 1.0 if (s - 4) < jb else 0.0
                ptab[p, 0, qt * 8 + s] = 0.0 if valid else -1e9
                ptab[p, 1, qt * 8 + s] = valid
                ptab[p, 2, qt * 8 + s] = 1.0 if s == 4 + jb else 0.0
    kk, qq = np.meshgrid(np.arange(128), np.arange(128), indexing="ij")
    tri = (qq >= kk).astype(bf)
    return {"kaugA": kaugA.astype(bf), "qaugA": qaug, "kaugB": kaugB.astype(bf), "qaugB": qaug.copy(),
            "ptab": ptab, "tri": tri, "ident": np.eye(128, dtype=np.float32)}


def _make_in_maps(inp):
    f = lambda a: np.ascontiguousarray(np.asarray(a, dtype=np.float32))
    x = f(inp["x"])
    cst = np.zeros((128, 384), np.float32)
    for j, nm in enumerate(("g_ffn1", "g_mix", "g_ffn2")):
        cst[:, 16 * j:16 * j + 16] = f(inp[nm])[0].reshape(16, 128).T
    cst[:, 48:64] = f(inp["g_final"]).reshape(16, 128).T
    cst[:, 64] = f(inp["g_subln"])[0]
    for j, nm in enumerate(("lam_q1", "lam_k1", "lam_q2", "lam_k2")):
        cst[:, 128 + 64 * j:192 + 64 * j] = f(inp[nm])[0][None, :]
    shared = {
        "w_gu1": f(inp["w_ffn1_gu"])[0], "w_dn1": f(inp["w_ffn1_down"])[0], "w_in": f(inp["w_in"])[0],
        "p_a": f(inp["p_a"])[0], "p_b": f(inp["p_b"])[0], "w_o": f(inp["w_o"])[0],
        "w_gu2": f(inp["w_ffn2_gu"])[0], "w_dn2": f(inp["w_ffn2_down"])[0], "cst": cst,
    }
    tabs = [_tables(0), _tables(1)]
    maps = []
    for c in range(8):
        b, half = c // 2, c % 2
        m = dict(shared)
        m.update(tabs[half])
        m["xT"] = np.ascontiguousarray(x[b, half * 1024:(half + 1) * 1024, :].T)
        m["xcT"] = np.ascontiguousarray(x[b, 0:1024, :].T)
        maps.append(m)
    return maps


def kernel(**inputs):
    maps = _make_in_maps(inputs)
    nc = build(debug=False)
    res = run_bass_kernel_spmd(nc, maps, core_ids=list(range(8)))
    out = np.empty((4, 2048, D), np.float32)
    for c in range(8):
        b, half = c // 2, c % 2
        out[b, half * 1024:(half + 1) * 1024, :] = np.asarray(res.results[c]["outT"], dtype=np.float32).T
    return out
```
